# Optimizing a Trainium2 kernel written in Bass

```python
import jax, jax.numpy as jnp
from jax import lax
import numpy as np

D_MODEL = 1024
BATCH = 16
SEQ = 2048
DEPTH = 2

CHUNK = 64
EPS = 1e-6
POOL_WINDOWS = (2, 4, 8, 16)
POOL_GROUP = D_MODEL // 4
POOL_WIDTH = 4 * POOL_GROUP
SGU_LEN = 128
SGU_HEAD_DIM = 128
SGU_WIDTH = D_MODEL
SGU_HEADS = SGU_WIDTH // SGU_HEAD_DIM
GLA_HEADS = 4
GLA_KW = D_MODEL // 2
GLA_VW = D_MODEL
GLA_DK = GLA_KW // GLA_HEADS
GLA_DV = GLA_VW // GLA_HEADS
GLA_RANK = 16
GLA_TAU = 16.0
SB_HEAD_DIM = 128
SB_WIDTH = D_MODEL
SB_HEADS = SB_WIDTH // SB_HEAD_DIM
SB_QBLOCK = 128
EVEN_SIZES = (POOL_WIDTH, SGU_WIDTH, SGU_WIDTH, POOL_WIDTH + SGU_WIDTH)
ODD_SIZES = (GLA_KW, GLA_KW, GLA_VW, GLA_RANK, SB_WIDTH, SB_WIDTH, SB_WIDTH, GLA_VW + SB_WIDTH)
EVEN_IN = sum(EVEN_SIZES)
ODD_IN = sum(ODD_SIZES)
EVEN_MIX = POOL_WIDTH + SGU_WIDTH
ODD_MIX = GLA_VW + SB_WIDTH
N_EVEN = (DEPTH + 1) // 2
N_ODD = DEPTH // 2

kernel_name = "hybrid_pool_sgu_gla_stickbreak_adaln"


def _split(t, sizes):
    idx = [int(i) for i in np.cumsum(sizes)[:-1]]
    return jnp.split(t, idx, axis=-1)


def rmsnorm(x, g):
    xf = x.astype(jnp.float32)
    y = xf * lax.rsqrt(jnp.mean(xf * xf, axis=-1, keepdims=True) + EPS)
    return (y * g.astype(jnp.float32)).astype(x.dtype)


def ada_modulate(x, c, norm_g, ada_w, ada_b):
    ada = jax.nn.silu(c) @ ada_w + ada_b
    shift, scale, gate = jnp.split(ada, 3, axis=-1)
    h = rmsnorm(x, norm_g) * (1.0 + scale[:, None, :]) + shift[:, None, :]
    return h, gate[:, None, :]


def pool_mixer(a, pool_w, pool_scale):
    B_, S, _ = a.shape
    af = a.astype(jnp.float32)
    cs = jnp.concatenate([jnp.zeros_like(af[:, :1]), jnp.cumsum(af, axis=1)], axis=1)
    t = np.arange(S)
    outs = []
    for g, w in enumerate(POOL_WINDOWS):
        sl = slice(g * POOL_GROUP, (g + 1) * POOL_GROUP)
        start = np.maximum(t + 1 - w, 0)
        count = jnp.asarray((t + 1 - start).astype(np.float32))
        win_sum = cs[:, 1:, sl] - cs[:, start, sl]
        outs.append(win_sum / count[None, :, None] - af[..., sl])
    p = jnp.stack(outs, axis=2).astype(a.dtype)
    y = jnp.einsum('bsgc,gcd->bsgd', p, pool_w).reshape(B_, S, POOL_WIDTH)
    return y * pool_scale


def spatial_gating(u, v, norm_g, w_s, b_s):
    B_, S, _ = u.shape
    n = S // SGU_LEN
    vh = rmsnorm(v, norm_g).reshape(B_, n, SGU_LEN, SGU_HEADS, SGU_HEAD_DIM)
    mask = jnp.tril(jnp.ones((SGU_LEN, SGU_LEN), dtype=bool))
    w = jnp.where(mask[None], w_s, jnp.zeros_like(w_s))
    z = jnp.einsum('hts,bnshd->bnthd', w, vh) + b_s.T[None, None, :, :, None]
    return u * z.reshape(B_, S, SGU_WIDTH)


def gla(q, k, v, log_a):
    B_, S = q.shape[:2]
    n = S // CHUNK
    r = lambda t: t.astype(jnp.float32).reshape(B_, n, CHUNK, *t.shape[2:])
    qf = r(q) * (GLA_DK ** -0.5)
    kf, vf, la = r(k), r(v), r(log_a)
    bcum = jnp.cumsum(la, axis=2)
    b_last = bcum[:, :, -1]
    q_dec = qf * jnp.exp(bcum)
    k_inv = kf * jnp.exp(-bcum)
    k_end = kf * jnp.exp(b_last[:, :, None] - bcum)
    mask = jnp.tril(jnp.ones((CHUNK, CHUNK), dtype=bool))
    att = jnp.einsum('bnthk,bnshk->bnhts', q_dec, k_inv)
    att = jnp.where(mask, att, 0.0)
    o_intra = jnp.einsum('bnhts,bnshv->bnthv', att, vf)

    def step(state, xs):
        qd, ke, vc, bl = xs
        o = jnp.einsum('bthk,bhkv->bthv', qd, state)
        state = jnp.exp(bl)[..., None] * state + jnp.einsum('bshk,bshv->bhkv', ke, vc)
        return state, o

    s0 = jnp.zeros((B_, GLA_HEADS, GLA_DK, GLA_DV), jnp.float32)
    xs = (jnp.moveaxis(q_dec, 1, 0), jnp.moveaxis(k_end, 1, 0), jnp.moveaxis(vf, 1, 0), jnp.moveaxis(b_last, 1, 0))
    _, o_inter = lax.scan(step, s0, xs)
    o = o_intra + jnp.moveaxis(o_inter, 0, 1)
    return o.reshape(B_, S, GLA_HEADS, GLA_DV).astype(v.dtype)


def stick_breaking(q, k, v):
    B_, S, H, Dh = q.shape
    scale = Dh ** -0.5
    outs = []
    for i in range(S // SB_QBLOCK):
        q0, q1 = i * SB_QBLOCK, (i + 1) * SB_QBLOCK
        z = jnp.einsum('bthd,bshd->bhts', q[:, q0:q1], k[:, :q1]).astype(jnp.float32) * scale
        strict = np.arange(q1)[None, :] < np.arange(q0, q1)[:, None]
        log_beta = jax.nn.log_sigmoid(z)
        log_1m = jnp.where(strict, jax.nn.log_sigmoid(-z), 0.0)
        after = lax.cumsum(log_1m, axis=3, reverse=True) - log_1m
        w = jnp.where(strict, jnp.exp(log_beta + after), 0.0)
        outs.append(jnp.einsum('bhts,bshd->bthd', w.astype(v.dtype), v[:, :q1]))
    return jnp.concatenate(outs, axis=1)


def even_layer(x, c, norm_g, ada_w, ada_b, in_w, pool_w, pool_scale, sgu_norm_g, sgu_w, sgu_b, out_w):
    h, gate_res = ada_modulate(x, c, norm_g, ada_w, ada_b)
    a, u, v, gate = _split(h @ in_w, EVEN_SIZES)
    ya = pool_mixer(a, pool_w, pool_scale)
    yb = spatial_gating(u, v, sgu_norm_g, sgu_w, sgu_b)
    y = jnp.concatenate([ya, yb], axis=-1) * jax.nn.silu(gate)
    return x + gate_res * (y @ out_w)


def odd_layer(x, c, norm_g, ada_w, ada_b, in_w, gate_w, gate_b, gla_norm_g, out_w):
    B_, S, _ = x.shape
    h, gate_res = ada_modulate(x, c, norm_g, ada_w, ada_b)
    gq, gk, gv, glr, sq, sk, sv, gate = _split(h @ in_w, ODD_SIZES)
    log_a = jax.nn.log_sigmoid((glr @ gate_w + gate_b).astype(jnp.float32)) / GLA_TAU
    hd = lambda t, n, d: t.reshape(B_, S, n, d)
    yc = gla(hd(gq, GLA_HEADS, GLA_DK), hd(gk, GLA_HEADS, GLA_DK), hd(gv, GLA_HEADS, GLA_DV),
             hd(log_a, GLA_HEADS, GLA_DK))
    yc = rmsnorm(yc, gla_norm_g).reshape(B_, S, GLA_VW)
    yd = stick_breaking(hd(sq, SB_HEADS, SB_HEAD_DIM), hd(sk, SB_HEADS, SB_HEAD_DIM),
                        hd(sv, SB_HEADS, SB_HEAD_DIM)).reshape(B_, S, SB_WIDTH)
    y = jnp.concatenate([yc, yd], axis=-1) * jax.nn.silu(gate)
    return x + gate_res * (y @ out_w)


def setup_inputs(seed: int = 0) -> dict:
    key = jax.random.key(seed)
    ks = jax.random.split(key, 20)
    nrm = lambda k, shape, s: jax.random.normal(k, shape, jnp.float32) * s
    return {
        "x": nrm(ks[0], (BATCH, SEQ, D_MODEL), 1.0),
        "c": nrm(ks[1], (BATCH, D_MODEL), 1.0),
        "ada_w": nrm(ks[2], (DEPTH, D_MODEL, 3 * D_MODEL), 0.5 * D_MODEL ** -0.5),
        "ada_b": nrm(ks[3], (DEPTH, 3 * D_MODEL), 0.02),
        "norm_g": 1.0 + nrm(ks[4], (DEPTH, D_MODEL), 0.05),
        "even_in_w": nrm(ks[5], (N_EVEN, D_MODEL, EVEN_IN), D_MODEL ** -0.5),
        "pool_w": nrm(ks[6], (N_EVEN, 4, POOL_GROUP, POOL_GROUP), POOL_GROUP ** -0.5),
        "pool_scale": 1.0 + nrm(ks[7], (N_EVEN, POOL_WIDTH), 0.1),
        "sgu_norm_g": 1.0 + nrm(ks[8], (N_EVEN, SGU_WIDTH), 0.05),
        "sgu_w": nrm(ks[9], (N_EVEN, SGU_HEADS, SGU_LEN, SGU_LEN), SGU_LEN ** -0.5),
        "sgu_b": 1.0 + nrm(ks[10], (N_EVEN, SGU_HEADS, SGU_LEN), 0.1),
        "even_out_w": nrm(ks[11], (N_EVEN, EVEN_MIX, D_MODEL), EVEN_MIX ** -0.5),
        "odd_in_w": nrm(ks[12], (N_ODD, D_MODEL, ODD_IN), D_MODEL ** -0.5),
        "gla_gate_w": nrm(ks[13], (N_ODD, GLA_RANK, GLA_KW), GLA_RANK ** -0.5),
        "gla_gate_b": nrm(ks[14], (N_ODD, GLA_KW), 0.1),
        "gla_norm_g": 1.0 + nrm(ks[15], (N_ODD, GLA_HEADS, GLA_DV), 0.05),
        "odd_out_w": nrm(ks[16], (N_ODD, ODD_MIX, D_MODEL), ODD_MIX ** -0.5),
        "final_g": 1.0 + nrm(ks[17], (D_MODEL,), 0.05),
    }


def reference(x, c, ada_w, ada_b, norm_g, even_in_w, pool_w, pool_scale, sgu_norm_g, sgu_w, sgu_b,
              even_out_w, odd_in_w, gla_gate_w, gla_gate_b, gla_norm_g, odd_out_w, final_g):
    for i in range(DEPTH):
        j = i // 2
        if i % 2 == 0:
            x = even_layer(x, c, norm_g[i], ada_w[i], ada_b[i], even_in_w[j], pool_w[j], pool_scale[j],
                           sgu_norm_g[j], sgu_w[j], sgu_b[j], even_out_w[j])
        else:
            x = odd_layer(x, c, norm_g[i], ada_w[i], ada_b[i], odd_in_w[j], gla_gate_w[j], gla_gate_b[j],
                          gla_norm_g[j], odd_out_w[j])
    return rmsnorm(x, final_g)
```

```python
from contextlib import ExitStack
import types
import numpy as np
import concourse.bass as bass
import concourse.mybir as mybir
from concourse.bass_utils import run_bass_kernel_spmd

F32 = mybir.dt.float32
BF16 = mybir.dt.bfloat16
AF = mybir.ActivationFunctionType
ALU = mybir.AluOpType
AX = mybir.AxisListType

ENGS = ('pe', 'act', 'dve', 'pool', 'sp')


def _freeze(fn):
    if fn.__closure__ is None:
        return fn
    cells = []
    for c in fn.__closure__:
        try:
            cells.append(types.CellType(c.cell_contents))
        except ValueError:
            cells.append(c)
    return types.FunctionType(fn.__code__, fn.__globals__, fn.__name__, fn.__defaults__, tuple(cells))


class _Op:
    __slots__ = ('eng', 'idx', 'fn', 'waits', 'signaled', 'sigcount', 'dma_key', 'dma_n', 'vc')

    def __init__(self, eng, idx, fn):
        self.eng = eng
        self.idx = idx
        self.fn = fn
        self.waits = []
        self.signaled = False
        self.sigcount = 0
        self.dma_key = None
        self.dma_n = 0
        self.vc = None


class Prog:
    def __init__(self, nc):
        self.nc = nc
        self.stack = ExitStack()
        self.streams = {e: [] for e in ENGS}
        self.know = {e: {} for e in ENGS}
        self.regs = {}
        self.dma_count = {}
        self.nops = 0
        self.pending = {e: [] for e in ENGS}
        self.last_dma = {}

    def sbuf(self, name, shape, dtype):
        return self.stack.enter_context(self.nc.sbuf_tensor(name, list(shape), dtype))

    def psum(self, name, shape, dtype):
        return self.stack.enter_context(self.nc.psum_tensor(name, list(shape), dtype))

    def _deps(self, eng, reads, writes):
        deps = []
        for k in reads:
            st = self.regs.get(k)
            if st is None:
                continue
            if st[0] is not None:
                deps.append(st[0])
            if isinstance(k, tuple) and k[0] == 'psum':
                for r in st[1]:
                    if r.eng != eng:
                        deps.append(r)
        for k in writes:
            st = self.regs.get(k)
            if st is None:
                continue
            if st[0] is not None:
                deps.append(st[0])
            deps.extend(st[1])
        return deps

    def _commit(self, op, reads, writes):
        for k in reads:
            st = self.regs.setdefault(k, [None, []])
            st[1].append(op)
        for k in writes:
            self.regs[k] = [op, []]

    def _record(self, eng, fn, reads, writes, dma_key=None):
        stream = self.streams[eng]
        op = _Op(eng, len(stream) + 1, _freeze(fn))
        K = self.know[eng]
        deps = self._deps(eng, reads, writes)
        if self.pending[eng]:
            deps = self.pending[eng] + deps
            self.pending[eng] = []
        for p in deps:
            if p.dma_key is not None:
                kk = ('dma', p.dma_key)
                need = p.dma_n
            else:
                if p.eng == 'pe' and eng == 'pe':
                    continue
                kk = p.eng
                need = p.idx
            if K.get(kk, 0) >= need:
                continue
            op.waits.append(p)
            p.signaled = True
            for a, b in p.vc.items():
                if K.get(a, 0) < b:
                    K[a] = b
        vc = dict(K)
        if dma_key is not None:
            n = self.dma_count.get(dma_key, 0) + 1
            self.dma_count[dma_key] = n
            op.dma_key = dma_key
            op.dma_n = n
            vc[('dma', dma_key)] = n
            self.last_dma[dma_key] = op
        else:
            vc[eng] = op.idx
            if eng != 'pe':
                pass
        op.vc = vc
        stream.append(op)
        self._commit(op, reads, writes)
        self.nops += 1
        return op

    def barrier(self):
        lasts = []
        for e in ENGS:
            for op in reversed(self.streams[e]):
                if op.dma_key is None:
                    lasts.append(op)
                    break
        lasts += list(self.last_dma.values())
        for e in ENGS:
            self.pending[e] = list(lasts)

    def op(self, eng, fn, reads=(), writes=()):
        return self._record(eng, fn, list(reads), list(writes))

    def dma(self, eng, out, in_, reads=(), writes=(), key=None):
        assert key is not None
        return self._record(eng, lambda e: e.dma_start(out=out, in_=in_), list(reads), list(writes), dma_key=key)

    def finish(self):
        nc = self.nc
        st = self.stack
        for e in ENGS:
            c = 0
            for op in self.streams[e]:
                if op.dma_key is None and op.signaled:
                    c += 1
                    op.sigcount = c
        esem = {e: st.enter_context(nc.semaphore('sem_' + e)) for e in ENGS}
        dsem = {}
        for k in self.dma_count:
            dsem[k] = st.enter_context(nc.semaphore('dsem_%d' % len(dsem)))
        final_dma = dict(self.dma_count)

        def emit(eng_name, e):
            for op in self.streams[eng_name]:
                for p in op.waits:
                    if p.dma_key is not None:
                        e.wait_ge(dsem[p.dma_key], 16 * p.dma_n)
                    else:
                        e.wait_ge(esem[p.eng], p.sigcount)
                ins = op.fn(e)
                if op.dma_key is not None:
                    ins.then_inc(dsem[op.dma_key], 16)
                elif op.signaled:
                    ins.then_inc(esem[eng_name], 1)
            if eng_name == 'sp':
                for k, n in final_dma.items():
                    e.wait_ge(dsem[k], 16 * n)

        with nc.Block() as block:
            @block.sync
            def _(e):
                emit('sp', e)

            @block.tensor
            def _(e):
                emit('pe', e)

            @block.scalar
            def _(e):
                emit('act', e)

            @block.vector
            def _(e):
                emit('dve', e)

            @block.gpsimd
            def _(e):
                emit('pool', e)
        st.close()


D = 1024
EPS = 1e-6
POOL_WINDOWS = (2, 4, 8, 16)
EVEN_IN = 5120
ODD_IN = 7184
NEG = -30000.0


class MK:
    def __init__(self, NSEQ=2, S=2048, nlayers=2, final=True):
        self.NSEQ, self.S, self.nlayers, self.final = NSEQ, S, nlayers, final
        self.NT = S // 128
        self.NG = S // 512
        nc = bass.Bass("TRN2", target_bir_lowering=False)
        self.nc = nc
        dt = nc.dram_tensor
        self.d = {}
        def inp(name, shape):
            self.d[name] = dt(name, list(shape), F32, kind="ExternalInput").ap()
        inp("x", [NSEQ, S, D]); inp("c", [NSEQ, D]); inp("ada_w", [2, D, 3 * D]); inp("ada_b", [2, 3 * D])
        inp("norm_g", [2, D]); inp("even_in_w", [D, EVEN_IN]); inp("pool_w", [4, 256, 256]); inp("pool_scale", [D])
        inp("sgu_norm_g", [D]); inp("sgu_wT", [8, 128, 128]); inp("sgu_b", [8 * 128]); inp("even_out_w", [2 * D, D])
        inp("odd_in_w", [D, ODD_IN]); inp("gla_gate_w", [16, 512]); inp("gla_gate_b", [512]); inp("gla_norm_g", [D])
        inp("odd_out_w", [2 * D, D]); inp("final_g", [D])
        self.out = dt("out", [NSEQ, S, D], F32, kind="ExternalOutput").ap()
        self.xs = dt("xs_scratch", [NSEQ, S, D], F32, kind="Internal").ap()
        self.P = Prog(nc)
        self.bankrr = 0
        self.marks = []
        self.alloc()

    def mark(self, name):
        self.marks.append((name, len(self.P.streams['pe'])))

    ARENA_BYTES = 64 * 1024

    def alloc(self):
        P, S, NSEQ = self.P, self.S, self.NSEQ
        sb = P.sbuf
        self.ident_f = sb("ident_f", [128, 128], F32)
        self.ident_b = sb("ident_b", [128, 128], BF16)
        self.banks = [P.psum("bank%d" % i, [128, 512], F32) for i in range(8)]
        self.hT = sb("hT", [128, 8, S], BF16)
        self.yT = sb("yT", [128, 8, S], BF16)
        self.ow = sb("ow", [128, 8, D], BF16)
        self.NB = 6
        self.wring = [sb("wr%d" % i, [128, 8, 256], BF16) for i in range(self.NB)]
        self.wri = 0
        self.xt = [sb("xt%d" % i, [128, D], F32) for i in range(2)]
        self.junk = sb("junk", [128, D], BF16)
        self.ss = [sb("ss%d" % i, [128, 2], F32) for i in range(2)]
        self.rstd = [sb("rstd%d" % i, [128, 1], F32) for i in range(2)]
        self.diag = [sb("diag%d" % i, [128, 128], F32) for i in range(2)]
        self.tt = [sb("tt%d" % i, [128, 512], F32) for i in range(2)]
        self.sg = [sb("sg%d" % i, [128, 512], F32) for i in range(2)]
        self.g1row = sb("g1row", [NSEQ, D], F32)
        self.vcols = sb("vcols", [128, 8, 8], F32)
        self.scT = sb("scT", [128, 8, NSEQ], BF16)
        self.abT = [sb("abT%d" % l, [128, 16, NSEQ], F32) for l in range(2)]
        self.A = [sb("A%d" % l, [128, NSEQ, 8], F32) for l in range(2)]
        self.Gb = [sb("Gb%d" % b, [128, D], F32) for b in range(NSEQ)]
        self.selb = [sb("selb%d" % b, [NSEQ, 128], F32) for b in range(NSEQ)]
        self.Gfin = sb("Gfin", [128, D], F32)
        self.arena = sb("arena", [128, self.ARENA_BYTES // 4], F32)
        self.ar_off = 0
        self.vrows = self.carve([1024], F32, part=8)
        self.abk = [self.carve([512], F32, part=NSEQ) for q in range(2)]
        self.arow = self.carve([2, 512], F32, part=NSEQ)
        self.nrr = 8
        self.wr_owner = [None] * self.NB
        self.item_done = set()
        self.item_ctr = 0

    def carve(self, shape_free, dtype, part=128):
        n = 1
        for v in shape_free:
            n *= v
        nbytes = n * (4 if dtype == F32 else 2)
        n4 = (nbytes + 3) // 4
        o = self.ar_off
        self.ar_off += n4
        assert self.ar_off * 4 <= self.ARENA_BYTES, ("arena overflow", self.ar_off * 4)
        ap = self.arena[0:part, o:o + n4]
        if dtype != F32:
            ap = ap.bitcast(dtype)
        if len(shape_free) == 2:
            ap = ap.rearrange("p (a b) -> p a b", a=shape_free[0])
        elif len(shape_free) == 3:
            ap = ap.rearrange("p (a b c) -> p a b c", a=shape_free[0], b=shape_free[1])
        return ap

    def bank(self):
        i = self.bankrr % self.nrr
        self.bankrr = (i + 1) % self.nrr
        return i

    def load_w(self, parts):
        i = self.wri
        self.wri = (i + 1) % self.NB
        buf = self.wring[i]
        assert self.wr_owner[i] is None or self.wr_owner[i] in self.item_done, "weight ring too small"
        self.wr_owner[i] = self.cur_issue
        o = 0
        for pi, (src, n) in enumerate(parts):
            self.P.dma('pool', buf[:, :, o:o + n], src.rearrange("(k p) f -> p k f", p=128),
                       writes=[('wr', i, pi)], key=('wr', i, pi))
            o += n
        assert o <= 256
        return buf, [('wr', i, pi) for pi in range(len(parts))]

    def run_items(self, items, PF=1, hook=None):
        loaded = {}
        ids = {}

        def issue(k):
            self.item_ctr += 1
            ids[k] = self.cur_issue = self.item_ctr
            loaded[k] = [self.load_w(parts) for parts in items[k][0]]
        for k in range(min(PF, len(items))):
            issue(k)
        for k in range(len(items)):
            if k + PF < len(items):
                issue(k + PF)
            if k == 1 and hook is not None:
                hook()
                hook = None
            items[k][1](loaded.pop(k))
            self.item_done.add(ids[k])
        if hook is not None:
            hook()

    def prologue(self):
        P, d, NSEQ = self.P, self.d, self.NSEQ
        idf, idb = self.ident_f, self.ident_b
        P.op('pool', lambda e: e.memset(idf[:], 0.0), writes=['ident_f'])
        P.op('pool', lambda e: e.affine_select(idf[:], idf[:], pattern=[[-1, 128]], compare_op=ALU.not_equal,
                                               fill=1.0, base=0, channel_multiplier=1),
             reads=['ident_f'], writes=['ident_f'])
        P.op('pool', lambda e: e.tensor_copy(idb[:], idf[:]), reads=['ident_f'], writes=['ident_b'])
        vr = self.vrows
        P.op('pool', lambda e: e.memset(vr, 0.0), writes=['vrows'])
        P.dma('sp', vr[0:NSEQ, :], d["c"], reads=['vrows'], writes=[('vrows', 0)], key=('vrows', 0))
        rows = {2: d["norm_g"][0], 3: d["norm_g"][1], 4: d["pool_scale"], 5: d["sgu_norm_g"], 6: d["gla_norm_g"]}
        for r, src in rows.items():
            P.dma('sp', vr[r:r + 1, :], src.rearrange("(o n) -> o n", o=1), reads=['vrows'], writes=[('vrows', r)],
                  key=('vrows', r))
        bk = self.bank()
        pb = self.banks[bk]
        for c in range(8):
            P.op('pe', lambda e: e.matmul(pb[:, c * 8:(c + 1) * 8], vr[0:8, c * 128:(c + 1) * 128], idf[0:8, 0:8],
                                          start=True, stop=True),
                 reads=['ident_f', 'vrows', ('vrows', 0)] + [('vrows', r) for r in rows], writes=[('psum', bk)])
        vc = self.vcols
        P.op('dve', lambda e: e.tensor_copy(vc[:].rearrange("p a b -> p (a b)"), pb[:, 0:64]),
             reads=[('psum', bk)], writes=['vcols'])
        scT = self.scT
        P.op('act', lambda e: e.activation(scT[:], vc[:, :, 0:NSEQ], AF.Silu), reads=['vcols'], writes=['scT'])
        for b in range(NSEQ):
            sel = self.selb[b]
            P.op('dve', lambda e: e.tensor_scalar(sel[:], idf[0:NSEQ, :], 0.0, idf[0:NSEQ, b:b + 1],
                                                  op0=ALU.mult, op1=ALU.add),
                 reads=['ident_f'], writes=[('selb', b)])
        gf = self.Gfin
        fg = d["final_g"]
        P.dma('sp', gf[:], bass.AP(fg.tensor, fg.offset, [[0, 128], [1, D]]), writes=['Gfin'], key='Gfin')
        self.ada_layer(0)

    def ada_layer(self, l):
        P, d, NSEQ = self.P, self.d, self.NSEQ
        scT, idf, arow, g1row, abT = self.scT, self.ident_f, self.arow, self.g1row, self.abT[l]
        ab = d["ada_b"][l]

        def blk(n):
            def fn(ws):
                q = n % 2
                abk = self.abk[q]
                P.dma('sp', abk, bass.AP(ab.tensor, ab.offset + n * 512, [[0, NSEQ], [1, 512]]),
                      writes=[('abk', q)], key=('abk', q))
                bk = self.bank()
                pb = self.banks[bk]
                for half in range(2):
                    buf, keys = ws[half]
                    for kc in range(8):
                        P.op('pe', lambda e: e.matmul(
                            pb[0:NSEQ, half * 256:(half + 1) * 256], scT[:, kc, :], buf[:, kc, 0:256],
                            start=(kc == 0), stop=(kc == 7)),
                            reads=['scT'] + keys, writes=[('psum', bk)])
                P.op('dve', lambda e: e.tensor_tensor(arow[:, q, :], pb[0:NSEQ, :], abk, op=ALU.add),
                     reads=[('psum', bk), ('abk', q)], writes=[('arow', q)])
                if n < 4:
                    bk2 = self.bank()
                    pb2 = self.banks[bk2]
                    for cc in range(4):
                        P.op('pe', lambda e: e.matmul(pb2[:, cc * NSEQ:(cc + 1) * NSEQ],
                                                      arow[0:NSEQ, q, cc * 128:(cc + 1) * 128],
                                                      idf[0:NSEQ, 0:NSEQ], start=True, stop=True),
                             reads=['ident_f', ('arow', q)], writes=[('psum', bk2)])
                    P.op('dve', lambda e: e.tensor_copy(abT[:, 4 * n:4 * n + 4, :].rearrange("p a b -> p (a b)"),
                                                        pb2[:, 0:4 * NSEQ]),
                         reads=[('psum', bk2)], writes=[('abT', l, n)])
                elif l == 0:
                    self.gate_bcast(arow[0:NSEQ, q, :], ('arow', q), n - 4)
                else:
                    P.op('pool', lambda e: e.tensor_copy(g1row[:, (n - 4) * 512:(n - 3) * 512], arow[0:NSEQ, q, :]),
                         reads=[('arow', q)], writes=[('g1row', n - 4)])
            return fn
        items = []
        for n in range(6):
            items.append(([[(d["ada_w"][l][:, n * 512 + h * 256:n * 512 + (h + 1) * 256], 256)] for h in range(2)],
                          blk(n)))
        self.run_items(items)
        A = self.A[l]
        vc = self.vcols
        for b in range(NSEQ):
            P.op('dve', lambda e: e.tensor_scalar(A[:, b, :], abT[:, 8:16, b], 1.0, 1.0, op0=ALU.add, op1=ALU.mult),
                 reads=[('abT', l, 2), ('abT', l, 3)], writes=[('A', l, b)])
            P.op('dve', lambda e: e.tensor_tensor(A[:, b, :], A[:, b, :], vc[:, :, 2 + l], op=ALU.mult),
                 reads=[('A', l, b), 'vcols'], writes=[('A', l, b)])

    def gate_bcast(self, rowsrc, rowkey, n):
        P, NSEQ = self.P, self.NSEQ
        for b in range(NSEQ):
            Gb, sel = self.Gb[b], self.selb[b]
            bk = self.bank()
            pb = self.banks[bk]
            P.op('pe', lambda e: e.matmul(pb[:, :], sel[0:NSEQ, :], rowsrc, start=True, stop=True),
                 reads=[('selb', b), rowkey], writes=[('psum', bk)])
            P.op('act', lambda e: e.activation(Gb[:, n * 512:(n + 1) * 512], pb[:, :], AF.Copy),
                 reads=[('psum', bk)], writes=[('Gb', b, n)])

    def frontend(self, l, b, xsrc, xkey):
        P, NT = self.P, self.NT
        idf, hT, junk = self.ident_f, self.hT, self.junk
        A, abT = self.A[l], self.abT[l]
        bks = {}

        def load(i):
            s = i % 2
            P.dma('sp', self.xt[s][:], xsrc[i * 128:(i + 1) * 128, :], reads=[(xkey, b, i)], writes=[('xt', s)],
                  key=('xt', s))

        def stats(i):
            s = i % 2
            xt, ss, rstd, diag = self.xt[s], self.ss[s], self.rstd[s], self.diag[s]
            P.op('act', lambda e: e.activation(junk[:], xt[:], AF.Square, scale=1.0 / 32, accum_out=ss[:, 0:1]),
                 reads=[('xt', s)], writes=['junk', ('ss', s)])
            P.op('act', lambda e: e.activation(rstd[:], ss[:, 0:1], AF.Ln, bias=EPS),
                 reads=[('ss', s)], writes=[('rstd', s)])
            P.op('act', lambda e: e.activation(rstd[:], rstd[:], AF.Exp, scale=-0.5),
                 reads=[('rstd', s)], writes=[('rstd', s)])
            P.op('dve', lambda e: e.tensor_scalar(diag[:], idf[:], rstd[:, 0:1], 1.0, op0=ALU.mult, op1=ALU.mult),
                 reads=[('rstd', s), 'ident_f'], writes=[('diag', s)])

        def trans(i):
            s = i % 2
            xt, diag = self.xt[s], self.diag[s]
            bks[i] = [self.bank(), self.bank()]
            for c in range(8):
                bk = bks[i][c // 4]
                pb = self.banks[bk]
                P.op('pe', lambda e: e.matmul(pb[:, (c % 4) * 128:(c % 4 + 1) * 128], xt[:, c * 128:(c + 1) * 128],
                                              diag[:], start=True, stop=True),
                     reads=[('xt', s), ('diag', s)], writes=[('psum', bk)])

        def evac(i):
            for c in range(8):
                bk = bks[i][c // 4]
                pb = self.banks[bk]
                src = pb[:, (c % 4) * 128:(c % 4 + 1) * 128]
                dst = hT[:, c, i * 128:(i + 1) * 128]
                rd = [('psum', bk), ('A', l, b), ('abT', l, 0), ('abT', l, 1)]
                wr = [('hT', c, i)]
                if c % 4 == 3:
                    P.op('act', lambda e: e.activation(dst, src, AF.Identity, scale=A[:, b, c:c + 1],
                                                       bias=abT[:, c, b:b + 1]), reads=rd, writes=wr)
                else:
                    P.op('dve', lambda e: e.tensor_scalar(dst, src, A[:, b, c:c + 1], abT[:, c, b:b + 1],
                                                          op0=ALU.mult, op1=ALU.add), reads=rd, writes=wr)
        load(0)
        if NT > 1:
            load(1)
        stats(0)
        trans(0)
        for i in range(NT):
            if i + 1 < NT:
                stats(i + 1)
            if i + 2 < NT:
                load(i + 2)
            if i + 1 < NT:
                trans(i + 1)
            evac(i)

    def hT_keys(self, t0, t1):
        return [('hT', c, i) for c in range(8) for i in range(t0, t1)]

    def load_ow(self, src):
        v = src.rearrange("(j p) f -> p j f", p=128)
        for h in range(2):
            self.P.dma('pool', self.ow[:, 4 * h:4 * h + 4, :], v[:, 4 * h:4 * h + 4, :], writes=[('ow', h)],
                       key=('ow', h))
        return self.ow, [('ow', 0), ('ow', 1)]

    def outproj(self, l, b, ow, owkeys, xsrc, xkey_r, dst, xkey_w, final):
        P, NT = self.P, self.NT
        yT, Gb, junk, gf = self.yT, self.Gb[b], self.junk, self.Gfin

        def load(i):
            s = i % 2
            P.dma('sp', self.xt[s][:], xsrc[i * 128:(i + 1) * 128, :], reads=[(xkey_r, b, i)], writes=[('xt', s)],
                  key=('xt', s))
        load(0)
        for i in range(NT):
            if i + 1 < NT:
                load(i + 1)
            s = i % 2
            xt = self.xt[s]
            for dh in range(2):
                bk = self.bank()
                pb = self.banks[bk]
                for j in range(8):
                    P.op('pe', lambda e, pb=pb, j=j, dh=dh: e.matmul(
                        pb[:, :], yT[:, j, i * 128:(i + 1) * 128], ow[:, j, dh * 512:(dh + 1) * 512],
                        start=(j == 0), stop=(j == 7)),
                        reads=[('yT', j, i), owkeys[j // 4]], writes=[('psum', bk)])
                tt = self.tt[dh]
                P.op('dve', lambda e, pb=pb, tt=tt, dh=dh: e.tensor_tensor(
                    tt[:], pb[:, :], Gb[:, dh * 512:(dh + 1) * 512], op=ALU.mult),
                    reads=[('psum', bk), ('Gb', b, dh)], writes=[('tt', dh)])
                P.op('dve', lambda e, xt=xt, tt=tt, dh=dh: e.tensor_tensor(
                    xt[:, dh * 512:(dh + 1) * 512], tt[:], xt[:, dh * 512:(dh + 1) * 512], op=ALU.add),
                    reads=[('tt', dh), ('xt', s)], writes=[('xt', s)])
            if final:
                ss, rstd = self.ss[s], self.rstd[s]
                P.op('act', lambda e, xt=xt, ss=ss: e.activation(junk[:], xt[:], AF.Square, scale=1.0 / 32,
                                                                accum_out=ss[:, 0:1]),
                     reads=[('xt', s)], writes=['junk', ('ss', s)])
                P.op('act', lambda e, ss=ss, rstd=rstd: e.activation(rstd[:], ss[:, 0:1], AF.Ln, bias=EPS),
                     reads=[('ss', s)], writes=[('rstd', s)])
                P.op('act', lambda e, rstd=rstd: e.activation(rstd[:], rstd[:], AF.Exp, scale=-0.5),
                     reads=[('rstd', s)], writes=[('rstd', s)])
                P.op('dve', lambda e, xt=xt, rstd=rstd: e.scalar_tensor_tensor(
                    xt[:], xt[:], rstd[:, 0:1], gf[:], op0=ALU.mult, op1=ALU.mult),
                    reads=[('xt', s), ('rstd', s), 'Gfin'], writes=[('xt', s)])
            P.dma('sp', dst[i * 128:(i + 1) * 128, :], xt[:], reads=[('xt', s)], writes=[(xkey_w, b, i)],
                  key=('xst', s))

    def even_consts(self):
        P, d = self.P, self.d
        cv = self.carve
        self.wTm = cv([8, 128], BF16)
        self.bB = cv([8 * 128], F32)
        self.poolw = cv([4, 2, 256], BF16)
        self.invc = cv([4, 16], F32)
        self.iot = cv([16], F32)
        self.aT = [cv([32 + 512], F32) for i in range(2)]
        self.t1 = cv([528], F32)
        self.t2 = cv([528], F32)
        self.t16 = cv([16], F32)
        self.t1b = cv([528], F32)
        self.t2b = cv([528], F32)
        self.sg4 = [cv([512], F32) for i in range(4)]
        self.pT = [[cv([512], BF16) for k in range(2)] for q in range(2)]
        self.vn = [cv([D], BF16) for i in range(2)]
        self.wT_f = cv([8, 128], F32)
        wTf, wTm, bB, poolw, invc, iot = self.wT_f, self.wTm, self.bB, self.poolw, self.invc, self.iot
        P.dma('sp', wTf, d["sgu_wT"].rearrange("h s t -> s h t"), writes=['wT_f'], key='wT_f')
        for h in range(8):
            P.op('pool', lambda e: e.affine_select(wTf[:, h, :], wTf[:, h, :], pattern=[[1, 128]],
                                                  compare_op=ALU.is_ge, fill=0.0, base=0, channel_multiplier=-1),
                 reads=['wT_f'], writes=['wT_f'])
        P.op('pool', lambda e: e.tensor_copy(wTm, wTf), reads=['wT_f'], writes=['wTm'])
        sbv = d["sgu_b"]
        P.dma('sp', bB, bass.AP(sbv.tensor, sbv.offset, [[0, 128], [1, 1024]]), writes=['bB'], key='bB')
        P.dma('pool', poolw, d["pool_w"].rearrange("g (k p) d -> p g k d", p=128), writes=['poolw'], key='poolw')
        P.op('pool', lambda e: e.iota(iot, pattern=[[1, 16]], base=1, channel_multiplier=0,
                                      allow_small_or_imprecise_dtypes=True), writes=['iot'])
        for g, w in enumerate(POOL_WINDOWS):
            P.op('dve', lambda e: e.tensor_scalar(invc[:, g, :], iot, float(w), 1.0, op0=ALU.min, op1=ALU.mult),
                 reads=['iot'], writes=[('invc', g)])
            P.op('dve', lambda e: e.reciprocal(invc[:, g, :], invc[:, g, :]), reads=[('invc', g)],
                 writes=[('invc', g)])

    def even_half1(self, b, hook):
        P, d, NG = self.P, self.d, self.NG
        hT, yT, vc = self.hT, self.yT, self.vcols
        W = d["even_in_w"]
        t1, t2, t16, invc, poolw = self.t1, self.t2, self.t16, self.invc, self.poolw
        banks = self.banks
        work = []

        def S1(u):
            g, tg, q, wa, ka = u['g'], u['tg'], u['q'], u['wa'], u['ka']
            w = POOL_WINDOWS[g]
            hk = self.hT_keys(4 * tg, 4 * tg + 4)
            for jj in range(2):
                aT = self.aT[jj]
                eng = 'pool' if jj == 0 else 'dve'
                if tg == 0:
                    P.op('pool', lambda e: e.memset(aT[:, 0:32], 0.0), writes=[('aT', jj)])
                else:
                    P.op(eng, lambda e: e.tensor_copy(aT[:, 0:32], aT[:, 512:544]), reads=[('aT', jj)],
                         writes=[('aT', jj)])
                bk = self.bank()
                pb = banks[bk]
                for kc in range(8):
                    P.op('pe', lambda e: e.matmul(pb[:, :], wa[:, kc, jj * 128:(jj + 1) * 128],
                                                  hT[:, kc, tg * 512:(tg + 1) * 512], start=(kc == 0), stop=(kc == 7)),
                         reads=ka + hk, writes=[('psum', bk)])
                P.op('act', lambda e: e.activation(aT[:, 32:544], pb[:, :], AF.Copy),
                     reads=[('psum', bk), ('aT', jj)], writes=[('aT', jj)])
                ta, tb = (t1, t2) if jj == 0 else (self.t1b, self.t2b)
                ka_, kb_ = ('t1', jj), ('t2', jj)
                P.op(eng, lambda e: e.tensor_tensor(ta[:, 0:528], aT[:, 16:544], aT[:, 15:543], op=ALU.add),
                     reads=[('aT', jj)], writes=[ka_])
                cur, ck = ta, ka_
                if w >= 4:
                    P.op(eng, lambda e: e.tensor_tensor(tb[:, 2:528], ta[:, 2:528], ta[:, 0:526], op=ALU.add),
                         reads=[ka_], writes=[kb_])
                    cur, ck = tb, kb_
                if w >= 8:
                    P.op(eng, lambda e: e.tensor_tensor(ta[:, 6:528], tb[:, 6:528], tb[:, 2:524], op=ALU.add),
                         reads=[kb_], writes=[ka_])
                    cur, ck = ta, ka_
                if w >= 16:
                    P.op(eng, lambda e: e.tensor_tensor(tb[:, 14:528], ta[:, 14:528], ta[:, 6:520], op=ALU.add),
                         reads=[ka_], writes=[kb_])
                    cur, ck = tb, kb_
                pT = self.pT[q][jj]
                P.op('dve', lambda e: e.scalar_tensor_tensor(pT[:, :], cur[:, 16:528], 1.0 / w, aT[:, 32:544],
                                                             op0=ALU.mult, op1=ALU.subtract),
                     reads=[ck, ('aT', jj)], writes=[('pT', q, jj)])
                if tg == 0:
                    P.op('dve', lambda e: e.tensor_tensor(t16[:], cur[:, 16:32], invc[:, g, :], op=ALU.mult),
                         reads=[ck, ('invc', g)], writes=['t16'])
                    P.op('dve', lambda e: e.tensor_tensor(pT[:, 0:16], t16[:], aT[:, 32:48], op=ALU.subtract),
                         reads=['t16', ('aT', jj), ('pT', q, jj)], writes=[('pT', q, jj)])

        def S2(u):
            g, tg, wg, kg = u['g'], u['tg'], u['wg'], u['kg']
            hk = self.hT_keys(4 * tg, 4 * tg + 4)
            for jo in range(2):
                bkg = self.bank()
                pbg = banks[bkg]
                for kc in range(8):
                    P.op('pe', lambda e: e.matmul(pbg[:, :], wg[:, kc, jo * 128:(jo + 1) * 128],
                                                  hT[:, kc, tg * 512:(tg + 1) * 512], start=(kc == 0), stop=(kc == 7)),
                         reads=kg + hk, writes=[('psum', bkg)])
                sg = self.sg4[2 * u['q'] + jo]
                P.op('act', lambda e: e.activation(sg, pbg[:, :], AF.Silu), reads=[('psum', bkg)],
                     writes=[('sg4', 2 * u['q'] + jo)])

        def S3(u):
            g, tg, q = u['g'], u['tg'], u['q']
            for jo in range(2):
                j = 2 * g + jo
                bkp = self.bank()
                pbp = banks[bkp]
                for kc in range(2):
                    pT = self.pT[q][kc]
                    P.op('pe', lambda e: e.matmul(pbp[:, :], poolw[:, g, kc, jo * 128:(jo + 1) * 128], pT[:, :],
                                                  start=(kc == 0), stop=(kc == 1)),
                         reads=['poolw', ('pT', q, kc)], writes=[('psum', bkp)])
                sg = self.sg4[2 * q + jo]
                P.op('dve', lambda e: e.scalar_tensor_tensor(yT[:, j, tg * 512:(tg + 1) * 512], pbp[:, :],
                                                             vc[:, j, 4:5], sg, op0=ALU.mult, op1=ALU.mult),
                     reads=[('psum', bkp), ('sg4', 2 * q + jo), 'vcols'],
                     writes=[('yT', j, i) for i in range(4 * tg, 4 * tg + 4)])

        specs = [[[(W[:, 256 * g:256 * g + 256], 256)], [(W[:, 3072 + 256 * g:3072 + 256 * g + 256], 256)]]
                 for g in range(4)]
        ids = {}

        def load(g):
            self.item_ctr += 1
            self.cur_issue = ids[g] = self.item_ctr
            return [self.load_w(parts) for parts in specs[g]]
        ws = {0: load(0)}
        units = []
        for g in range(4):
            for tg in range(NG):
                units.append(dict(g=g, tg=tg, q=len(units) % 2))
        n = len(units)

        def bind(u):
            (wa, ka), (wg, kg) = ws[u['g']]
            u.update(wa=wa, ka=ka, wg=wg, kg=kg)
        bind(units[0])
        S1(units[0])
        for k, u in enumerate(units):
            if u['tg'] == 0 and u['g'] + 1 < 4:
                ws[u['g'] + 1] = load(u['g'] + 1)
            if k == 2 and hook is not None:
                hook()
            S2(u)
            S3(u)
            if k + 1 < n:
                bind(units[k + 1])
                S1(units[k + 1])
            if u['tg'] == NG - 1:
                self.item_done.add(ids[u['g']])
        if n <= 2 and hook is not None:
            hook()

    def even_half2(self, b, hook):
        P, d, NG, NT = self.P, self.d, self.NG, self.NT
        hT, yT, vc, junk = self.hT, self.yT, self.vcols, self.junk
        wTm, bB = self.wTm, self.bB
        W = d["even_in_w"]

        def vitem(ws):
            st = {}

            def V1(i):
                s = i % 2
                hk = self.hT_keys(i, i + 1)
                ss = self.ss[s]
                bks = [self.bank(), self.bank()]
                st[i] = bks
                for hf in range(2):
                    pb = self.banks[bks[hf]]
                    for q in range(2):
                        wv, kv = ws[2 * hf + q]
                        for kc in range(8):
                            P.op('pe', lambda e: e.matmul(
                                pb[:, q * 256:(q + 1) * 256], hT[:, kc, i * 128:(i + 1) * 128], wv[:, kc, 0:256],
                                start=(kc == 0), stop=(kc == 7)), reads=kv + hk, writes=[('psum', bks[hf])])

            def V1b(i):
                s = i % 2
                ss = self.ss[s]
                bks = st[i]
                for hf in range(2):
                    pb = self.banks[bks[hf]]
                    P.op('act', lambda e: e.activation(junk[:, 0:512], pb[:, :], AF.Square,
                                                       accum_out=ss[:, hf:hf + 1]),
                         reads=[('psum', bks[hf])], writes=['junk', ('ss', s)])

            def V2(i):
                s = i % 2
                ss, rstd, vn = self.ss[s], self.rstd[s], self.vn[s]
                bks = st[i]
                P.op('dve', lambda e: e.tensor_tensor(ss[:, 0:1], ss[:, 0:1], ss[:, 1:2], op=ALU.add),
                     reads=[('ss', s)], writes=[('ss', s)])
                P.op('act', lambda e: e.activation(rstd[:], ss[:, 0:1], AF.Ln, scale=1.0 / 1024, bias=EPS),
                     reads=[('ss', s)], writes=[('rstd', s)])
                P.op('act', lambda e: e.activation(rstd[:], rstd[:], AF.Exp, scale=-0.5),
                     reads=[('rstd', s)], writes=[('rstd', s)])
                pb0, pb1 = self.banks[bks[0]], self.banks[bks[1]]
                P.op('dve', lambda e: e.tensor_scalar(
                    vn[:, 0:512], pb0[:, :], rstd[:, 0:1], 1.0, op0=ALU.mult, op1=ALU.mult),
                    reads=[('psum', bks[0]), ('rstd', s)], writes=[('vn', s, 0)])
                P.op('act', lambda e: e.activation(vn[:, 512:1024], pb1[:, :], AF.Copy, scale=rstd[:, 0:1]),
                     reads=[('psum', bks[1]), ('rstd', s)], writes=[('vn', s, 1)])

            def V3(i):
                s = i % 2
                vn = self.vn[s]
                zb = [self.bank(), self.bank()]
                for hh in range(8):
                    pz = self.banks[zb[hh // 4]]
                    P.op('pe', lambda e: e.matmul(
                        pz[:, (hh % 4) * 128:(hh % 4 + 1) * 128], vn[:, hh * 128:(hh + 1) * 128], wTm[:, hh, :],
                        start=True, stop=True),
                        reads=[('vn', s, hh // 4), 'wTm'], writes=[('psum', zb[hh // 4])])
                for hh in range(8):
                    pz = self.banks[zb[hh // 4]]
                    P.op('dve', lambda e: e.scalar_tensor_tensor(
                        yT[:, hh, i * 128:(i + 1) * 128], pz[:, (hh % 4) * 128:(hh % 4 + 1) * 128], vc[:, hh, 5:6],
                        bB[:, hh * 128:(hh + 1) * 128], op0=ALU.mult, op1=ALU.add),
                        reads=[('psum', zb[hh // 4]), 'vcols', 'bB'], writes=[('yT', hh, i)])
            V1(0)
            V1b(0)
            for i in range(NT):
                if i + 1 < NT:
                    V1(i + 1)
                V2(i)
                if i + 1 < NT:
                    V1b(i + 1)
                V3(i)

        def uitem(hp):
            def fn(ws):
                (wu, ku), (wg, kg) = ws
                for tg in range(NG):
                    hk = self.hT_keys(4 * tg, 4 * tg + 4)
                    for jo in range(2):
                        hh = 2 * hp + jo
                        bku, bkg = self.bank(), self.bank()
                        pbu, pbg = self.banks[bku], self.banks[bkg]
                        for kc in range(8):
                            P.op('pe', lambda e: e.matmul(
                                pbu[:, :], wu[:, kc, jo * 128:(jo + 1) * 128], hT[:, kc, tg * 512:(tg + 1) * 512],
                                start=(kc == 0), stop=(kc == 7)), reads=ku + hk, writes=[('psum', bku)])
                        for kc in range(8):
                            P.op('pe', lambda e: e.matmul(
                                pbg[:, :], wg[:, kc, jo * 128:(jo + 1) * 128], hT[:, kc, tg * 512:(tg + 1) * 512],
                                start=(kc == 0), stop=(kc == 7)), reads=kg + hk, writes=[('psum', bkg)])
                        sg, tt = self.sg[jo], self.tt[jo]
                        P.op('act', lambda e: e.activation(sg[:], pbg[:, :], AF.Silu),
                             reads=[('psum', bkg)], writes=[('sg', jo)])
                        P.op('dve', lambda e: e.tensor_tensor(tt[:], pbu[:, :], sg[:], op=ALU.mult),
                             reads=[('psum', bku), ('sg', jo)], writes=[('tt', jo)])
                        yk = [('yT', hh, i) for i in range(4 * tg, 4 * tg + 4)]
                        P.op('pool', lambda e: e.tensor_tensor(
                            yT[:, hh, tg * 512:(tg + 1) * 512], tt[:], yT[:, hh, tg * 512:(tg + 1) * 512], op=ALU.mult),
                            reads=[('tt', jo)] + yk, writes=yk)
            return fn
        items = [([[(W[:, 2048 + 256 * q:2048 + 256 * q + 256], 256)] for q in range(4)], vitem)]
        for hp in range(4):
            items.append(([[(W[:, 1024 + 256 * hp:1024 + 256 * hp + 256], 256)],
                           [(W[:, 4096 + 256 * hp:4096 + 256 * hp + 256], 256)]], uitem(hp)))
        self.run_items(items, hook=hook)

    def odd_consts(self):
        P, d, S, NT = self.P, self.d, self.S, self.NT
        self.ar_off = 0
        cv = self.carve
        self.M1f = cv([128], F32); self.M2f = cv([128], F32)
        self.negTri = cv([128], BF16); self.negOnes = cv([128], BF16); self.ones_b = cv([128], BF16)
        self.NEGm = cv([4, 512], BF16)
        self.gw = cv([512], F32, part=16); self.gbrow = cv([512], F32, part=1); self.ones1 = cv([128], F32, part=1)
        self.wglr = cv([8, 16], BF16)
        self.qdT = cv([S], BF16); self.kiT = cv([S], BF16)
        self.kend = cv([NT, 128], BF16); self.gv = cv([NT, 256], BF16)
        self.f32 = [cv([512], F32) for _ in range(5)]
        self.b16 = [cv([512], BF16) for _ in range(7)]
        self.S32 = cv([256], F32); self.dec = cv([NT], F32); self.glrT = cv([512], F32, part=16)
        self.qT2 = cv([S], BF16); self.kT2 = cv([S], BF16); self.vv1 = cv([NT, 128], BF16)
        M1f, M2f, negTri, negOnes, ones_b, NEGm = self.M1f, self.M2f, self.negTri, self.negOnes, self.ones_b, self.NEGm
        t0, t1 = self.f32[0], self.f32[1]
        P.op('pool', lambda e: e.memset(M1f, 1.0), writes=['M1f'])
        P.op('pool', lambda e: e.affine_select(M1f, M1f, pattern=[[1, 128]], compare_op=ALU.is_ge, fill=0.0, base=0,
                                               channel_multiplier=-1), reads=['M1f'], writes=['M1f'])
        P.op('pool', lambda e: e.memset(M2f, 1.0), writes=['M2f'])
        P.op('pool', lambda e: e.affine_select(M2f, M2f, pattern=[[-1, 128]], compare_op=ALU.is_gt, fill=0.0, base=0,
                                               channel_multiplier=1), reads=['M2f'], writes=['M2f'])
        P.op('pool', lambda e: e.memset(t1[:, 0:128], -1.0), writes=[('f32', 1)])
        P.op('pool', lambda e: e.affine_select(t1[:, 0:128], t1[:, 0:128], pattern=[[-1, 128]], compare_op=ALU.is_ge,
                                               fill=0.0, base=0, channel_multiplier=1),
             reads=[('f32', 1)], writes=[('f32', 1)])
        P.op('pool', lambda e: e.tensor_copy(negTri, t1[:, 0:128]), reads=[('f32', 1)], writes=['negTri'])
        P.op('pool', lambda e: e.memset(t1[:, 0:128], -1.0), reads=[('f32', 1)], writes=[('f32', 1)])
        P.op('pool', lambda e: e.tensor_copy(negOnes, t1[:, 0:128]), reads=[('f32', 1)], writes=['negOnes'])
        P.op('pool', lambda e: e.memset(t1[:, 0:128], 1.0), reads=[('f32', 1)], writes=[('f32', 1)])
        P.op('pool', lambda e: e.tensor_copy(ones_b, t1[:, 0:128]), reads=[('f32', 1)], writes=['ones_b'])
        for a in range(4):
            P.op('pool', lambda e: e.memset(t0, 0.0), reads=[('f32', 0)], writes=[('f32', 0)])
            P.op('pool', lambda e: e.affine_select(t0, t0, pattern=[[1, 512]], compare_op=ALU.is_gt, fill=NEG,
                                                   base=-a * 128, channel_multiplier=-1),
                 reads=[('f32', 0)], writes=[('f32', 0)])
            P.op('pool', lambda e: e.tensor_copy(NEGm[:, a, :], t0), reads=[('f32', 0)], writes=[('NEGm', a)])
        P.dma('sp', self.gw, d["gla_gate_w"], writes=['gw'], key='gw')
        P.dma('sp', self.gbrow, d["gla_gate_b"].rearrange("(o n) -> o n", o=1), writes=['gbrow'], key='gbrow')
        P.op('pool', lambda e: e.memset(self.ones1, 1.0), writes=['ones1'])
        P.dma('pool', self.wglr, d["odd_in_w"][:, 2048:2064].rearrange("(k p) f -> p k f", p=128), writes=['wglr'],
              key='wglr')

    def odd_half1(self, b, hook):
        P, d, NG, NT = self.P, self.d, self.NG, self.NT
        hT, yT, vc = self.hT, self.yT, self.vcols
        W = d["odd_in_w"]
        M1f, M2f, ones_b, gw, gbrow, ones1, wglr = self.M1f, self.M2f, self.ones_b, self.gw, self.gbrow, self.ones1, self.wglr
        qdT, kiT, kend, gv, S32, dec, glrT = self.qdT, self.kiT, self.kend, self.gv, self.S32, self.dec, self.glrT
        Ed, Ei, Es, sp4, rsb = self.f32
        osq = [self.b16[0], self.b16[1]]
        attm = [self.b16[3][:, 0:128], self.b16[3][:, 128:256]]
        Sb = [self.b16[5][:, 0:256], self.b16[5][:, 256:512]]
        banks = self.banks
        QS = 128 ** -0.5

        e1 = self.junk[:, :].bitcast(F32)

        def step1_pieces(h, tg, wqk, kqk, wv, kv):
            hk = self.hT_keys(4 * tg, 4 * tg + 4)
            st = {}

            def p_glr():
                bk = self.bank(); pb = banks[bk]
                for kc in range(8):
                    P.op('pe', lambda e: e.matmul(pb[0:16, :], wglr[:, kc, :], hT[:, kc, tg * 512:(tg + 1) * 512],
                                                  start=(kc == 0), stop=(kc == 7)), reads=['wglr'] + hk,
                         writes=[('psum', bk)])
                P.op('act', lambda e: e.activation(glrT, pb[0:16, :], AF.Copy), reads=[('psum', bk)], writes=['glrT'])

            def p_la():
                bk = self.bank(); pbL = banks[bk]
                for ti in range(4):
                    P.op('pe', lambda e: e.matmul(pbL[:, ti * 128:(ti + 1) * 128], glrT[0:16, ti * 128:(ti + 1) * 128],
                                                  gw[0:16, h * 128:(h + 1) * 128], start=True, stop=False),
                         reads=['glrT', 'gw'], writes=[('psum', bk)])
                    P.op('pe', lambda e: e.matmul(pbL[:, ti * 128:(ti + 1) * 128], ones1[0:1, :],
                                                  gbrow[0:1, h * 128:(h + 1) * 128], start=False, stop=True),
                         reads=['ones1', 'gbrow'], writes=[('psum', bk)])
                P.op('act', lambda e: e.activation(e1, pbL[:, :], AF.Exp, scale=-1.0), reads=[('psum', bk)],
                     writes=['junk'])
                P.op('act', lambda e: e.activation(sp4, e1, AF.Ln, bias=1.0), reads=['junk'], writes=[('f32', 3)])

            def p_bd():
                bkB = self.bank(); pbB = banks[bkB]
                for ti in range(4):
                    P.op('pe', lambda e: e.matmul(pbB[:, ti * 128:(ti + 1) * 128], sp4[:, ti * 128:(ti + 1) * 128], M1f,
                                                  start=True, stop=True), reads=[('f32', 3), 'M1f'],
                         writes=[('psum', bkB)])
                bkD = self.bank(); pbD = banks[bkD]
                for ti in range(4):
                    P.op('pe', lambda e: e.matmul(pbD[:, ti * 128:(ti + 1) * 128], M2f, sp4[:, ti * 128:(ti + 1) * 128],
                                                  start=True, stop=True), reads=[('f32', 3), 'M2f'],
                         writes=[('psum', bkD)])
                P.op('act', lambda e: e.activation(Ed, pbB[:, :], AF.Exp, scale=-1.0 / 16), reads=[('psum', bkB)],
                     writes=[('f32', 0)])
                P.op('act', lambda e: e.activation(Ei, pbB[:, :], AF.Exp, scale=1.0 / 16), reads=[('psum', bkB)],
                     writes=[('f32', 1)])
                P.op('act', lambda e: e.activation(Es, pbD[:, :], AF.Exp, scale=-1.0 / 16), reads=[('psum', bkD)],
                     writes=[('f32', 2)])
                P.op('pool', lambda e: e.tensor_copy(dec[:, 4 * tg:4 * tg + 4],
                                                     Ed.rearrange("p (a b) -> p a b", a=4)[:, :, 127]),
                     reads=[('f32', 0)], writes=[('dec', tg)])

            def p_q():
                bkQ = self.bank(); pbQ = banks[bkQ]
                for kc in range(8):
                    P.op('pe', lambda e: e.matmul(pbQ[:, :], wqk[:, kc, 0:128], hT[:, kc, tg * 512:(tg + 1) * 512],
                                                  start=(kc == 0), stop=(kc == 7)), reads=kqk + hk,
                         writes=[('psum', bkQ)])
                P.op('dve', lambda e: e.scalar_tensor_tensor(qdT[:, tg * 512:(tg + 1) * 512], pbQ[:, :], QS, Ed,
                                                             op0=ALU.mult, op1=ALU.mult),
                     reads=[('psum', bkQ), ('f32', 0)], writes=[('qdT', tg)])

            def p_k():
                bkK = self.bank(); pbK = banks[bkK]
                for kc in range(8):
                    P.op('pe', lambda e: e.matmul(pbK[:, :], wqk[:, kc, 128:256], hT[:, kc, tg * 512:(tg + 1) * 512],
                                                  start=(kc == 0), stop=(kc == 7)), reads=kqk + hk,
                         writes=[('psum', bkK)])
                P.op('dve', lambda e: e.tensor_tensor(kiT[:, tg * 512:(tg + 1) * 512], pbK[:, :], Ei, op=ALU.mult),
                     reads=[('psum', bkK), ('f32', 1)], writes=[('kiT', tg)])

            def p_t():
                bkT = self.bank(); pbT = banks[bkT]
                for ti in range(4):
                    i = 4 * tg + ti
                    for kc in range(8):
                        P.op('pe', lambda e: e.matmul(pbT[:, ti * 128:(ti + 1) * 128], hT[:, kc, i * 128:(i + 1) * 128],
                                                      wqk[:, kc, 128:256], start=(kc == 0), stop=(kc == 7)),
                             reads=kqk + hk, writes=[('psum', bkT)])
                P.op('dve', lambda e: e.tensor_tensor(kend[:, 4 * tg:4 * tg + 4, :].rearrange("p a b -> p (a b)"),
                                                      pbT[:, :], Es, op=ALU.mult),
                     reads=[('psum', bkT), ('f32', 2)], writes=[('kend', tg)])

            def p_v(hv):
                def f():
                    bkV = self.bank(); pbV = banks[bkV]
                    for t2 in range(2):
                        i = 4 * tg + 2 * hv + t2
                        for kc in range(8):
                            P.op('pe', lambda e: e.matmul(pbV[:, t2 * 256:(t2 + 1) * 256],
                                                          hT[:, kc, i * 128:(i + 1) * 128], wv[:, kc, 0:256],
                                                          start=(kc == 0), stop=(kc == 7)),
                                 reads=kv + hk, writes=[('psum', bkV)])
                    dst = gv[:, 4 * tg + 2 * hv:4 * tg + 2 * hv + 2, :].rearrange("p a b -> p (a b)")
                    P.op('dve', lambda e: e.tensor_copy(dst, pbV[:, :]), reads=[('psum', bkV)],
                         writes=[('gv', tg, hv)])
                return f
            return [p_glr, p_v(0), p_la, p_v(1), p_bd, p_q, p_k, p_t]

        def step23(h, tg, wg, kg, pieces):
            hk = self.hT_keys(4 * tg, 4 * tg + 4)
            O = [banks[6], banks[7]]
            for v2 in range(2):
                bkg = self.bank(); pbg = banks[bkg]
                for kc in range(8):
                    P.op('pe', lambda e: e.matmul(pbg[:, :], wg[:, kc, v2 * 128:(v2 + 1) * 128],
                                                  hT[:, kc, tg * 512:(tg + 1) * 512], start=(kc == 0), stop=(kc == 7)),
                         reads=kg + hk, writes=[('psum', bkg)])
                sg = self.sg[v2]
                self.silu_gate(pbg[:, :], ('psum', bkg), sg[:], ('sg', v2))
                P.op('dve', lambda e: e.tensor_tensor(sg[:], pbg[:, :], sg[:], op=ALU.mult),
                     reads=[('psum', bkg), ('sg', v2)], writes=[('sg', v2)])
            for ti in range(4):
                for _ in range(2 if ti < 2 else 1):
                    if pieces:
                        pieces.pop(0)()
                i = 4 * tg + ti
                q = i % 2
                bkA = self.bank(); pbA = banks[bkA]
                P.op('pe', lambda e: e.matmul(pbA[:, 0:128], kiT[:, i * 128:(i + 1) * 128], qdT[:, i * 128:(i + 1) * 128],
                                              start=True, stop=True), reads=[('kiT', tg), ('qdT', tg)],
                     writes=[('psum', bkA)])
                P.op('dve', lambda e: e.tensor_tensor(attm[q], pbA[:, 0:128], M1f, op=ALU.mult),
                     reads=[('psum', bkA), 'M1f'], writes=[('attm', q)])
                for v2 in range(2):
                    P.op('pe', lambda e: e.matmul(O[v2][:, ti * 128:(ti + 1) * 128], gv[:, i, v2 * 128:(v2 + 1) * 128],
                                                  attm[q], start=True, stop=False),
                         reads=[('gv', tg, ti // 2), ('attm', q)], writes=[('psum', 6 + v2)])
                    P.op('pe', lambda e: e.matmul(O[v2][:, ti * 128:(ti + 1) * 128], Sb[q][:, v2 * 128:(v2 + 1) * 128],
                                                  qdT[:, i * 128:(i + 1) * 128], start=False, stop=True),
                         reads=[('Sb', q), ('qdT', tg)], writes=[('psum', 6 + v2)])
                bkU = self.bank(); pbU = banks[bkU]
                P.op('pe', lambda e: e.matmul(pbU[:, 0:256], kend[:, i, :], gv[:, i, :], start=True, stop=True),
                     reads=[('kend', tg), ('gv', tg, ti // 2)], writes=[('psum', bkU)])
                P.op('dve', lambda e: e.scalar_tensor_tensor(S32, S32, dec[:, i:i + 1], pbU[:, 0:256],
                                                             op0=ALU.mult, op1=ALU.add),
                     reads=['S32', ('dec', tg), ('psum', bkU)], writes=['S32'])
                P.op('act', lambda e: e.activation(Sb[1 - q], S32, AF.Copy), reads=['S32'], writes=[('Sb', 1 - q)])
            while len(pieces) > 2:
                pieces.pop(0)()
            for v2 in range(2):
                P.op('act', lambda e: e.activation(osq[v2], O[v2][:, :], AF.Square), reads=[('psum', 6 + v2)],
                     writes=[('b16', v2)])
            if pieces:
                pieces.pop(0)()
            bkN = self.bank(); pbN = banks[bkN]
            for v2 in range(2):
                P.op('pe', lambda e: e.matmul(pbN[:, :], ones_b, osq[v2], start=(v2 == 0), stop=(v2 == 1)),
                     reads=['ones_b', ('b16', v2)], writes=[('psum', bkN)])
            P.op('act', lambda e: e.activation(rsb, pbN[:, :], AF.Ln, scale=1.0 / 256, bias=EPS), reads=[('psum', bkN)],
                 writes=[('f32', 4)])
            P.op('act', lambda e: e.activation(rsb, rsb, AF.Exp, scale=-0.5), reads=[('f32', 4)], writes=[('f32', 4)])
            while pieces:
                pieces.pop(0)()
            for v2 in range(2):
                j = 2 * h + v2
                sg, tt = self.sg[v2], self.tt[v2]
                P.op('dve', lambda e: e.scalar_tensor_tensor(tt[:], O[v2][:, :], vc[:, j, 6:7], rsb,
                                                             op0=ALU.mult, op1=ALU.mult),
                     reads=[('psum', 6 + v2), 'vcols', ('f32', 4)], writes=[('tt', v2)])
                yk = [('yT', j, i) for i in range(4 * tg, 4 * tg + 4)]
                P.op('dve', lambda e: e.tensor_tensor(yT[:, j, tg * 512:(tg + 1) * 512], tt[:], sg[:], op=ALU.mult),
                     reads=[('tt', v2), ('sg', v2)], writes=yk)

        def wspec(h):
            return [[(W[:, 128 * h:128 * h + 128], 128), (W[:, 512 + 128 * h:512 + 128 * h + 128], 128)],
                    [(W[:, 1024 + 256 * h:1024 + 256 * h + 256], 256)],
                    [(W[:, 5136 + 256 * h:5136 + 256 * h + 256], 256)]]
        ids = {}

        def load(h):
            self.item_ctr += 1
            self.cur_issue = ids[h] = self.item_ctr
            return [self.load_w(parts) for parts in wspec(h)]
        ws = {0: load(0)}
        self.nrr = 6
        self.bankrr = 0
        (wqk, kqk), (wv, kv), (wg, kg) = ws[0]
        for pc in step1_pieces(0, 0, wqk, kqk, wv, kv):
            pc()
        for h in range(4):
            (wqk, kqk), (wv, kv), (wg, kg) = ws[h]
            if h + 1 < 4:
                ws[h + 1] = load(h + 1)
            if h == 1 and hook is not None:
                hook()
            P.op('pool', lambda e: e.memset(S32, 0.0), writes=['S32'])
            P.op('pool', lambda e: e.memset(Sb[0], 0.0), writes=[('Sb', 0)])
            if NG == 1 and h > 0:
                for pc in step1_pieces(h, 0, wqk, kqk, wv, kv):
                    pc()
            for tg in range(NG):
                if tg + 1 < NG:
                    nxt = step1_pieces(h, tg + 1, wqk, kqk, wv, kv)
                elif h + 1 < 4 and NG > 1:
                    (wqk2, kqk2), (wv2, kv2), _ = ws[h + 1]
                    nxt = step1_pieces(h + 1, 0, wqk2, kqk2, wv2, kv2)
                else:
                    nxt = []
                step23(h, tg, wg, kg, nxt)
            self.item_done.add(ids[h])
        self.nrr = 8

    def silu_gate(self, pbG, gkey, sgbuf, sgkey):
        P = self.P
        P.op('act', lambda e: e.activation(sgbuf, pbG, AF.Exp, scale=-1.0), reads=[gkey], writes=[sgkey])
        P.op('act', lambda e: e.activation(sgbuf, sgbuf, AF.Ln, bias=1.0), reads=[sgkey], writes=[sgkey])
        P.op('act', lambda e: e.activation(sgbuf, sgbuf, AF.Exp, scale=-1.0), reads=[sgkey], writes=[sgkey])

    def odd_half2(self, b, hook):
        P, d, NG, NT = self.P, self.d, self.NG, self.NT
        hT, yT, idb = self.hT, self.yT, self.ident_b[:, :]
        W = d["odd_in_w"]
        negTri, negOnes, NEGm = self.negTri, self.negOnes, self.NEGm
        qTs = [self.qdT, self.qT2]
        kTs = [self.kiT, self.kT2]
        vvs = [self.gv[:, :, 0:128], self.vv1]
        eb = self.f32[0:3]
        xx = self.f32[3:5]
        Lb = self.b16[0:3]
        wb = self.b16[3:5]
        Rb = self.b16[5:7]
        banks = self.banks
        QS = 128 ** -0.5
        NEGtri = NEGm[:, 0, 0:128]

        def vkeys(p, tg):
            return [('gv', tg, 0), ('gv', tg, 1)] if p == 0 else [('vv1', tg)]

        def step1_pieces(hh, ws, bank_fn):
            (wqk, kqk), (wvg, kvg) = ws
            p = hh % 2
            qT, kT, vv = qTs[p], kTs[p], vvs[p]
            pieces = []
            for tg in range(NG):
                hk = self.hT_keys(4 * tg, 4 * tg + 4)
                st = {}

                def pqk(which, half, tg=tg, hk=hk, st=st):
                    def f():
                        if half == 0:
                            st[which] = bank_fn()
                        bk = st[which]; pb = banks[bk]
                        lo = 0 if which == 'q' else 128
                        for kc in range(4 * half, 4 * half + 4):
                            P.op('pe', lambda e: e.matmul(pb[:, :], wqk[:, kc, lo:lo + 128],
                                                          hT[:, kc, tg * 512:(tg + 1) * 512],
                                                          start=(kc == 0), stop=(kc == 7)), reads=kqk + hk,
                                 writes=[('psum', bk)])
                        if half == 1:
                            if which == 'q':
                                P.op('dve', lambda e: e.tensor_scalar(qT[:, tg * 512:(tg + 1) * 512], pb[:, :], QS, 1.0,
                                                                      op0=ALU.mult, op1=ALU.mult),
                                     reads=[('psum', bk)], writes=[('qT', p, tg)])
                            else:
                                P.op('dve', lambda e: e.tensor_copy(kT[:, tg * 512:(tg + 1) * 512], pb[:, :]),
                                     reads=[('psum', bk)], writes=[('kT', p, tg)])
                    return f

                def pv(half, tg=tg, hk=hk, st=st):
                    def f():
                        if half == 0:
                            st['v'] = bank_fn()
                        bk = st['v']; pb = banks[bk]
                        for ti in range(2 * half, 2 * half + 2):
                            i = 4 * tg + ti
                            for kc in range(8):
                                P.op('pe', lambda e: e.matmul(pb[:, ti * 128:(ti + 1) * 128],
                                                              hT[:, kc, i * 128:(i + 1) * 128], wvg[:, kc, 0:128],
                                                              start=(kc == 0), stop=(kc == 7)),
                                     reads=kvg + hk, writes=[('psum', bk)])
                        if half == 1:
                            P.op('dve', lambda e: e.tensor_copy(vv[:, 4 * tg:4 * tg + 4, :],
                                                                pb[:, :].rearrange("p (a b) -> p a b", a=4)),
                                 reads=[('psum', bk)], writes=vkeys(p, tg))
                    return f
                pieces += [pqk('q', 0), pqk('q', 1), pqk('k', 0), pqk('k', 1), pv(0), pv(1)]
            return pieces

        def step2(hh, ws, pieces):
            (wqk, kqk), (wvg, kvg) = ws
            p = hh % 2
            qT, kT, vv = qTs[p], kTs[p], vvs[p]
            tiles = []
            for qg in range(NG):
                jbs = list(range(4 * qg + 3, -1, -1))
                for n, jb in enumerate(jbs):
                    a = jb - 4 * qg
                    tiles.append(dict(qg=qg, n=n, N=len(jbs), jb=jb, c0=(a * 128 if a > 0 else 0), diag=(a >= 0)))
            T = len(tiles)
            for t, tl in enumerate(tiles):
                tl['t'] = t
                tl['prev'] = tiles[t - 1] if tl['n'] > 0 else None

            def stZ(tl):
                t, qg, jb, c0 = tl['t'], tl['qg'], tl['jb'], tl['c0']
                zb = t % 4
                pbZ = banks[zb]
                P.op('pe', lambda e: e.matmul(pbZ[:, c0:512], kT[:, jb * 128:(jb + 1) * 128],
                                              qT[:, qg * 512 + c0:(qg + 1) * 512], start=True, stop=(not tl['diag'])),
                     reads=[('kT', p, jb // 4), ('qT', p, qg)], writes=[('psum', zb)])
                if tl['diag']:
                    P.op('pe', lambda e: e.matmul(pbZ[:, c0:c0 + 128], idb, NEGtri, start=False, stop=True),
                         reads=['ident_b', ('NEGm', 0)], writes=[('psum', zb)])

            def stA(tl):
                t, n, N, c0 = tl['t'], tl['n'], tl['N'], tl['c0']
                zb = t % 4
                pbZ = banks[zb]
                e_, L_ = eb[t % 3], Lb[t % 3]
                P.op('act', lambda e: e.activation(e_[:, c0:512], pbZ[:, c0:512], AF.Exp), reads=[('psum', zb)],
                     writes=[('f32', t % 3)])
                P.op('act', lambda e: e.activation(L_[:, c0:512], e_[:, c0:512], AF.Ln, bias=1.0),
                     reads=[('f32', t % 3)], writes=[('b16', t % 3)])
                if n < N - 1:
                    R_ = Rb[t % 2]
                    rk = ('b16', 5 + t % 2)
                    if n == 0:
                        P.op('pool', lambda e: e.tensor_copy(R_[:, c0:512], L_[:, c0:512]), reads=[('b16', t % 3)],
                             writes=[rk])
                    else:
                        pc0 = tl['prev']['c0']
                        Rp = Rb[(t - 1) % 2]
                        P.op('pool', lambda e: e.tensor_tensor(R_[:, pc0:512], Rp[:, pc0:512], L_[:, pc0:512],
                                                               op=ALU.add),
                             reads=[('b16', 5 + (t - 1) % 2), ('b16', t % 3)], writes=[rk])
                        if pc0 > c0:
                            P.op('pool', lambda e: e.tensor_copy(R_[:, c0:pc0], L_[:, c0:pc0]),
                                 reads=[('b16', t % 3), rk], writes=[rk])

            def stC(tl):
                t, n, c0 = tl['t'], tl['n'], tl['c0']
                zb = t % 4
                pbZ = banks[zb]
                L_ = Lb[t % 3]
                P.op('pe', lambda e: e.matmul(pbZ[:, c0:512], negTri, L_[:, c0:512], start=False, stop=(n == 0),
                                              skip_group_check=True),
                     reads=['negTri', ('b16', t % 3)], writes=[('psum', zb)])
                if n >= 1:
                    pc0 = tl['prev']['c0']
                    Rp = Rb[(t - 1) % 2]
                    P.op('pe', lambda e: e.matmul(pbZ[:, pc0:512], negOnes, Rp[:, pc0:512], start=False, stop=True,
                                                  skip_group_check=True),
                         reads=['negOnes', ('b16', 5 + (t - 1) % 2)], writes=[('psum', zb)])

            def stX(tl):
                t, n, c0 = tl['t'], tl['n'], tl['c0']
                zb = t % 4
                pbZ = banks[zb]
                w_ = wb[t % 2]
                if n == 0 and c0 > 0:
                    P.op('pool', lambda e: e.memset(w_[:, 0:c0], 0.0), reads=[('b16', 3 + t % 2)],
                         writes=[('b16', 3 + t % 2)])
                P.op('act', lambda e: e.activation(w_[:, c0:512], pbZ[:, c0:512], AF.Exp),
                     reads=[('psum', zb), ('b16', 3 + t % 2)], writes=[('b16', 3 + t % 2)])

            def stO(tl):
                t, n, N, qg, jb, c0 = tl['t'], tl['n'], tl['N'], tl['qg'], tl['jb'], tl['c0']
                ob = 4 + (qg % 2)
                pbO = banks[ob]
                oc0 = 0 if n == 0 else c0
                P.op('pe', lambda e: e.matmul(pbO[:, oc0:512], vv[:, jb, :], wb[t % 2][:, oc0:512],
                                              start=(n == 0), stop=(n == N - 1)),
                     reads=vkeys(p, jb // 4) + [('b16', 3 + t % 2)], writes=[('psum', ob)])
                if n == N - 1:
                    hk = self.hT_keys(4 * qg, 4 * qg + 4)
                    pbG = banks[6]
                    for kc in range(8):
                        P.op('pe', lambda e: e.matmul(pbG[:, :], wvg[:, kc, 128:256], hT[:, kc, qg * 512:(qg + 1) * 512],
                                                      start=(kc == 0), stop=(kc == 7)), reads=kvg + hk,
                             writes=[('psum', 6)])
                    sg, tt = self.sg[qg % 2], self.tt[qg % 2]
                    self.silu_gate(pbG[:, :], ('psum', 6), sg[:], ('sg', qg % 2))
                    P.op('dve', lambda e: e.tensor_tensor(tt[:], pbG[:, :], sg[:], op=ALU.mult),
                         reads=[('psum', 6), ('sg', qg % 2)], writes=[('tt', qg % 2)])
                    yk = [('yT', hh, i) for i in range(4 * qg, 4 * qg + 4)]
                    P.op('dve', lambda e: e.tensor_tensor(yT[:, hh, qg * 512:(qg + 1) * 512], pbO[:, :], tt[:],
                                                          op=ALU.mult),
                         reads=[('psum', ob), ('tt', qg % 2)], writes=yk)

            pieces = list(pieces)
            npc = len(pieces)
            done_pc = [0]
            stZ(tiles[0])
            for m in range(T + 2):
                if 0 <= m - 1 < T:
                    stC(tiles[m - 1])
                if m + 1 < T:
                    stZ(tiles[m + 1])
                if 0 <= m - 2 < T:
                    stO(tiles[m - 2])
                if pieces and m >= 1:
                    want = min(npc, ((m * npc) // max(1, T - 3)) + 1)
                    while pieces and done_pc[0] < want:
                        pieces.pop(0)()
                        done_pc[0] += 1
                if m < T:
                    stA(tiles[m])
                if 0 <= m - 1 < T:
                    stX(tiles[m - 1])
            while pieces:
                pieces.pop(0)()

        def wspec(hh):
            return [[(W[:, 2064 + 128 * hh:2064 + 128 * hh + 128], 128),
                     (W[:, 3088 + 128 * hh:3088 + 128 * hh + 128], 128)],
                    [(W[:, 4112 + 128 * hh:4112 + 128 * hh + 128], 128),
                     (W[:, 6160 + 128 * hh:6160 + 128 * hh + 128], 128)]]

        def load(hh):
            self.item_ctr += 1
            self.cur_issue = self.item_ctr
            ids[hh] = self.item_ctr
            return [self.load_w(parts) for parts in wspec(hh)]
        ids = {}
        ws = {0: load(0)}
        self.nrr = 8
        for piece in step1_pieces(0, ws[0], self.bank):
            piece()
        for hh in range(8):
            pieces = []
            if hh + 1 < 8:
                ws[hh + 1] = load(hh + 1)
                pieces = step1_pieces(hh + 1, ws[hh + 1], lambda: 7)
            if hh == 1 and hook is not None:
                hook()
            step2(hh, ws[hh], pieces)
            self.item_done.add(ids[hh])

    def build(self):
        d, NSEQ = self.d, self.NSEQ
        self.mark('prologue')
        self.prologue()
        self.even_consts()
        one = (self.nlayers == 1)
        for b in range(NSEQ):
            self.mark('L0 s%d frontend' % b)
            self.frontend(0, b, d["x"][b], 'xin')
            st = {}
            self.mark('L0 s%d pool' % b)
            self.even_half1(b, lambda: st.update(ow=self.load_ow(d["even_out_w"][0:1024, :])))
            if b == 0 and not one:
                self.ada_layer(1)
            self.mark('L0 s%d out1' % b)
            self.outproj(0, b, st['ow'][0], st['ow'][1], d["x"][b], 'xin', self.xs[b], 'xs', final=False)
            self.mark('L0 s%d sgu' % b)
            self.even_half2(b, lambda: st.update(ow=self.load_ow(d["even_out_w"][1024:2048, :])))
            self.mark('L0 s%d out2' % b)
            self.outproj(0, b, st['ow'][0], st['ow'][1], self.xs[b], 'xs', self.out[b] if one else self.xs[b],
                         'out' if one else 'xs', final=(one and self.final))
        if not one:
            self.P.barrier()
            self.odd_consts()
            for n in range(2):
                self.gate_bcast(self.g1row[0:NSEQ, n * 512:(n + 1) * 512], ('g1row', n), n)
            for b in range(NSEQ):
                self.mark('L1 s%d frontend' % b)
                self.frontend(1, b, self.xs[b], 'xs')
                st = {}
                self.mark('L1 s%d gla' % b)
                self.odd_half1(b, lambda: st.update(ow=self.load_ow(d["odd_out_w"][0:1024, :])))
                self.mark('L1 s%d out1' % b)
                self.outproj(1, b, st['ow'][0], st['ow'][1], self.xs[b], 'xs', self.xs[b], 'xs', final=False)
                self.mark('L1 s%d sb' % b)
                self.odd_half2(b, lambda: st.update(ow=self.load_ow(d["odd_out_w"][1024:2048, :])))
                self.mark('L1 s%d out2' % b)
                self.outproj(1, b, st['ow'][0], st['ow'][1], self.xs[b], 'xs', self.out[b], 'out', final=self.final)
        self.mark('end')
        self.P.finish()
        return self.nc


def make_in_maps(inputs, NSEQ, ncores):
    f = lambda a: np.ascontiguousarray(np.asarray(a, dtype=np.float32))
    shared = {
        "ada_w": f(inputs["ada_w"]), "ada_b": f(inputs["ada_b"]), "norm_g": f(inputs["norm_g"]),
        "even_in_w": f(inputs["even_in_w"][0]), "pool_w": f(inputs["pool_w"][0]),
        "pool_scale": f(inputs["pool_scale"][0]), "sgu_norm_g": f(inputs["sgu_norm_g"][0]),
        "sgu_wT": f(np.transpose(np.asarray(inputs["sgu_w"][0]), (0, 2, 1))),
        "sgu_b": f(np.asarray(inputs["sgu_b"][0]).reshape(-1)), "even_out_w": f(inputs["even_out_w"][0]),
        "odd_in_w": f(inputs["odd_in_w"][0]), "gla_gate_w": f(inputs["gla_gate_w"][0]),
        "gla_gate_b": f(inputs["gla_gate_b"][0]), "gla_norm_g": f(np.asarray(inputs["gla_norm_g"][0]).reshape(-1)),
        "odd_out_w": f(inputs["odd_out_w"][0]), "final_g": f(inputs["final_g"]),
    }
    x = np.asarray(inputs["x"], dtype=np.float32)
    c = np.asarray(inputs["c"], dtype=np.float32)
    maps = []
    for i in range(ncores):
        m = dict(shared)
        m["x"] = np.ascontiguousarray(x[i * NSEQ:(i + 1) * NSEQ])
        m["c"] = np.ascontiguousarray(c[i * NSEQ:(i + 1) * NSEQ])
        maps.append(m)
    return maps


def kernel(**inputs):
    ncores, NSEQ = 8, 2
    mk = MK(NSEQ=NSEQ, S=2048, nlayers=2, final=True)
    nc = mk.build()
    maps = make_in_maps(inputs, NSEQ, ncores)
    res = run_bass_kernel_spmd(nc, maps, core_ids=list(range(ncores)))
    return np.concatenate([r["out"] for r in res.results], axis=0)
```

```python
from contextlib import ExitStack
import types
import numpy as np
import concourse.bass as bass
import concourse.mybir as mybir
from concourse.bass_utils import run_bass_kernel_spmd

F32 = mybir.dt.float32
BF16 = mybir.dt.bfloat16
AF = mybir.ActivationFunctionType
ALU = mybir.AluOpType
AX = mybir.AxisListType

ENGS = ('pe', 'act', 'dve', 'pool', 'sp')


def _freeze(fn):
    if fn.__closure__ is None:
        return fn
    cells = []
    for c in fn.__closure__:
        try:
            cells.append(types.CellType(c.cell_contents))
        except ValueError:
            cells.append(c)
    return types.FunctionType(fn.__code__, fn.__globals__, fn.__name__, fn.__defaults__, tuple(cells))


class _Op:
    __slots__ = ('eng', 'idx', 'fn', 'waits', 'signaled', 'sigcount', 'dma_key', 'dma_n', 'vc')

    def __init__(self, eng, idx, fn):
        self.eng = eng
        self.idx = idx
        self.fn = fn
        self.waits = []
        self.signaled = False
        self.sigcount = 0
        self.dma_key = None
        self.dma_n = 0
        self.vc = None


class Prog:
    def __init__(self, nc):
        self.nc = nc
        self.stack = ExitStack()
        self.streams = {e: [] for e in ENGS}
        self.know = {e: {} for e in ENGS}
        self.regs = {}
        self.dma_count = {}
        self.nops = 0
        self.pending = {e: [] for e in ENGS}
        self.last_dma = {}

    def sbuf(self, name, shape, dtype):
        return self.stack.enter_context(self.nc.sbuf_tensor(name, list(shape), dtype))

    def psum(self, name, shape, dtype):
        return self.stack.enter_context(self.nc.psum_tensor(name, list(shape), dtype))

    def _deps(self, eng, reads, writes):
        deps = []
        for k in reads:
            st = self.regs.get(k)
            if st is None:
                continue
            if st[0] is not None:
                deps.append(st[0])
            if isinstance(k, tuple) and k[0] == 'psum':
                for r in st[1]:
                    if r.eng != eng:
                        deps.append(r)
        for k in writes:
            st = self.regs.get(k)
            if st is None:
                continue
            if st[0] is not None:
                deps.append(st[0])
            deps.extend(st[1])
        return deps

    def _commit(self, op, reads, writes):
        for k in reads:
            st = self.regs.setdefault(k, [None, []])
            st[1].append(op)
        for k in writes:
            self.regs[k] = [op, []]

    def _record(self, eng, fn, reads, writes, dma_key=None):
        stream = self.streams[eng]
        op = _Op(eng, len(stream) + 1, _freeze(fn))
        K = self.know[eng]
        deps = self._deps(eng, reads, writes)
        if self.pending[eng]:
            deps = self.pending[eng] + deps
            self.pending[eng] = []
        for p in deps:
            if p.dma_key is not None:
                kk = ('dma', p.dma_key)
                need = p.dma_n
            else:
                if p.eng == 'pe' and eng == 'pe':
                    continue
                kk = p.eng
                need = p.idx
            if K.get(kk, 0) >= need:
                continue
            op.waits.append(p)
            p.signaled = True
            for a, b in p.vc.items():
                if K.get(a, 0) < b:
                    K[a] = b
        vc = dict(K)
        if dma_key is not None:
            n = self.dma_count.get(dma_key, 0) + 1
            self.dma_count[dma_key] = n
            op.dma_key = dma_key
            op.dma_n = n
            vc[('dma', dma_key)] = n
            self.last_dma[dma_key] = op
        else:
            vc[eng] = op.idx
            if eng != 'pe':
                pass
        op.vc = vc
        stream.append(op)
        self._commit(op, reads, writes)
        self.nops += 1
        return op

    def barrier(self):
        lasts = []
        for e in ENGS:
            for op in reversed(self.streams[e]):
                if op.dma_key is None:
                    lasts.append(op)
                    break
        lasts += list(self.last_dma.values())
        for e in ENGS:
            self.pending[e] = list(lasts)

    def op(self, eng, fn, reads=(), writes=()):
        return self._record(eng, fn, list(reads), list(writes))

    def dma(self, eng, out, in_, reads=(), writes=(), key=None):
        assert key is not None
        return self._record(eng, lambda e: e.dma_start(out=out, in_=in_), list(reads), list(writes), dma_key=key)

    def finish(self):
        nc = self.nc
        st = self.stack
        for e in ENGS:
            c = 0
            for op in self.streams[e]:
                if op.dma_key is None and op.signaled:
                    c += 1
                    op.sigcount = c
        esem = {e: st.enter_context(nc.semaphore('sem_' + e)) for e in ENGS}
        dsem = {}
        for k in self.dma_count:
            dsem[k] = st.enter_context(nc.semaphore('dsem_%d' % len(dsem)))
        final_dma = dict(self.dma_count)

        def emit(eng_name, e):
            for op in self.streams[eng_name]:
                for p in op.waits:
                    if p.dma_key is not None:
                        e.wait_ge(dsem[p.dma_key], 16 * p.dma_n)
                    else:
                        e.wait_ge(esem[p.eng], p.sigcount)
                ins = op.fn(e)
                if op.dma_key is not None:
                    ins.then_inc(dsem[op.dma_key], 16)
                elif op.signaled:
                    ins.then_inc(esem[eng_name], 1)
            if eng_name == 'sp':
                for k, n in final_dma.items():
                    e.wait_ge(dsem[k], 16 * n)

        with nc.Block() as block:
            @block.sync
            def _(e):
                emit('sp', e)

            @block.tensor
            def _(e):
                emit('pe', e)

            @block.scalar
            def _(e):
                emit('act', e)

            @block.vector
            def _(e):
                emit('dve', e)

            @block.gpsimd
            def _(e):
                emit('pool', e)
        st.close()


D = 1024
EPS = 1e-6
POOL_WINDOWS = (2, 4, 8, 16)
EVEN_IN = 5120
ODD_IN = 7184
NEG = -30000.0


class MK:
    def __init__(self, NSEQ=2, S=2048, nlayers=2, final=True):
        self.NSEQ, self.S, self.nlayers, self.final = NSEQ, S, nlayers, final
        self.NT = S // 128
        self.NG = S // 512
        nc = bass.Bass("TRN2", target_bir_lowering=False)
        self.nc = nc
        dt = nc.dram_tensor
        self.d = {}
        def inp(name, shape):
            self.d[name] = dt(name, list(shape), F32, kind="ExternalInput").ap()
        inp("x", [NSEQ, S, D]); inp("c", [NSEQ, D]); inp("ada_w", [2, D, 3 * D]); inp("ada_b", [2, 3 * D])
        inp("norm_g", [2, D]); inp("even_in_w", [D, EVEN_IN]); inp("pool_w", [4, 256, 256]); inp("pool_scale", [D])
        inp("sgu_norm_g", [D]); inp("sgu_wT", [8, 128, 128]); inp("sgu_b", [8 * 128]); inp("even_out_w", [2 * D, D])
        inp("odd_in_w", [D, ODD_IN]); inp("gla_gate_w", [16, 512]); inp("gla_gate_b", [512]); inp("gla_norm_g", [D])
        inp("odd_out_w", [2 * D, D]); inp("final_g", [D])
        self.out = dt("out", [NSEQ, S, D], F32, kind="ExternalOutput").ap()
        self.xs = dt("xs_scratch", [NSEQ, S, D], F32, kind="Internal").ap()
        self.P = Prog(nc)
        self.bankrr = 0
        self.marks = []
        self.alloc()

    def mark(self, name):
        self.marks.append((name, len(self.P.streams['pe'])))

    ARENA_BYTES = 64 * 1024

    def alloc(self):
        P, S, NSEQ = self.P, self.S, self.NSEQ
        sb = P.sbuf
        self.ident_f = sb("ident_f", [128, 128], F32)
        self.ident_b = sb("ident_b", [128, 128], BF16)
        self.banks = [P.psum("bank%d" % i, [128, 512], F32) for i in range(8)]
        self.hT = sb("hT", [128, 8, S], BF16)
        self.yT = sb("yT", [128, 8, S], BF16)
        self.ow = sb("ow", [128, 8, D], BF16)
        self.NB = 6
        self.wring = [sb("wr%d" % i, [128, 8, 256], BF16) for i in range(self.NB)]
        self.wri = 0
        self.xt = [sb("xt%d" % i, [128, D], F32) for i in range(2)]
        self.junk = sb("junk", [128, D], BF16)
        self.ss = [sb("ss%d" % i, [128, 2], F32) for i in range(2)]
        self.rstd = [sb("rstd%d" % i, [128, 1], F32) for i in range(2)]
        self.diag = [sb("diag%d" % i, [128, 128], F32) for i in range(2)]
        self.tt = [sb("tt%d" % i, [128, 512], F32) for i in range(2)]
        self.sg = [sb("sg%d" % i, [128, 512], F32) for i in range(2)]
        self.g1row = sb("g1row", [NSEQ, D], F32)
        self.vcols = sb("vcols", [128, 8, 8], F32)
        self.scT = sb("scT", [128, 8, NSEQ], BF16)
        self.abT = [sb("abT%d" % l, [128, 16, NSEQ], F32) for l in range(2)]
        self.A = [sb("A%d" % l, [128, NSEQ, 8], F32) for l in range(2)]
        self.Gb = [sb("Gb%d" % b, [128, D], F32) for b in range(NSEQ)]
        self.selb = [sb("selb%d" % b, [NSEQ, 128], F32) for b in range(NSEQ)]
        self.Gfin = sb("Gfin", [128, D], F32)
        self.arena = sb("arena", [128, self.ARENA_BYTES // 4], F32)
        self.ar_off = 0
        self.vrows = self.carve([1024], F32, part=8)
        self.abk = [self.carve([512], F32, part=NSEQ) for q in range(2)]
        self.arow = self.carve([2, 512], F32, part=NSEQ)
        self.nrr = 8
        self.wr_owner = [None] * self.NB
        self.item_done = set()
        self.item_ctr = 0

    def carve(self, shape_free, dtype, part=128):
        n = 1
        for v in shape_free:
            n *= v
        nbytes = n * (4 if dtype == F32 else 2)
        n4 = (nbytes + 3) // 4
        o = self.ar_off
        self.ar_off += n4
        assert self.ar_off * 4 <= self.ARENA_BYTES, ("arena overflow", self.ar_off * 4)
        ap = self.arena[0:part, o:o + n4]
        if dtype != F32:
            ap = ap.bitcast(dtype)
        if len(shape_free) == 2:
            ap = ap.rearrange("p (a b) -> p a b", a=shape_free[0])
        elif len(shape_free) == 3:
            ap = ap.rearrange("p (a b c) -> p a b c", a=shape_free[0], b=shape_free[1])
        return ap

    def bank(self):
        i = self.bankrr % self.nrr
        self.bankrr = (i + 1) % self.nrr
        return i

    def load_w(self, parts):
        i = self.wri
        self.wri = (i + 1) % self.NB
        buf = self.wring[i]
        assert self.wr_owner[i] is None or self.wr_owner[i] in self.item_done, "weight ring too small"
        self.wr_owner[i] = self.cur_issue
        o = 0
        for pi, (src, n) in enumerate(parts):
            self.P.dma('pool', buf[:, :, o:o + n], src.rearrange("(k p) f -> p k f", p=128),
                       writes=[('wr', i, pi)], key=('wr', i, pi))
            o += n
        assert o <= 256
        return buf, [('wr', i, pi) for pi in range(len(parts))]

    def run_items(self, items, PF=1, hook=None):
        loaded = {}
        ids = {}

        def issue(k):
            self.item_ctr += 1
            ids[k] = self.cur_issue = self.item_ctr
            loaded[k] = [self.load_w(parts) for parts in items[k][0]]
        for k in range(min(PF, len(items))):
            issue(k)
        for k in range(len(items)):
            if k + PF < len(items):
                issue(k + PF)
            if k == 1 and hook is not None:
                hook()
                hook = None
            items[k][1](loaded.pop(k))
            self.item_done.add(ids[k])
        if hook is not None:
            hook()

    def prologue(self):
        P, d, NSEQ = self.P, self.d, self.NSEQ
        idf, idb = self.ident_f, self.ident_b
        P.op('pool', lambda e: e.memset(idf[:], 0.0), writes=['ident_f'])
        P.op('pool', lambda e: e.affine_select(idf[:], idf[:], pattern=[[-1, 128]], compare_op=ALU.not_equal,
                                               fill=1.0, base=0, channel_multiplier=1),
             reads=['ident_f'], writes=['ident_f'])
        P.op('pool', lambda e: e.tensor_copy(idb[:], idf[:]), reads=['ident_f'], writes=['ident_b'])
        vr = self.vrows
        P.op('pool', lambda e: e.memset(vr, 0.0), writes=['vrows'])
        P.dma('sp', vr[0:NSEQ, :], d["c"], reads=['vrows'], writes=[('vrows', 0)], key=('vrows', 0))
        rows = {2: d["norm_g"][0], 3: d["norm_g"][1], 4: d["pool_scale"], 5: d["sgu_norm_g"], 6: d["gla_norm_g"]}
        for r, src in rows.items():
            P.dma('sp', vr[r:r + 1, :], src.rearrange("(o n) -> o n", o=1), reads=['vrows'], writes=[('vrows', r)],
                  key=('vrows', r))
        bk = self.bank()
        pb = self.banks[bk]
        for c in range(8):
            P.op('pe', lambda e: e.matmul(pb[:, c * 8:(c + 1) * 8], vr[0:8, c * 128:(c + 1) * 128], idf[0:8, 0:8],
                                          start=True, stop=True),
                 reads=['ident_f', 'vrows', ('vrows', 0)] + [('vrows', r) for r in rows], writes=[('psum', bk)])
        vc = self.vcols
        P.op('dve', lambda e: e.tensor_copy(vc[:].rearrange("p a b -> p (a b)"), pb[:, 0:64]),
             reads=[('psum', bk)], writes=['vcols'])
        scT = self.scT
        P.op('act', lambda e: e.activation(scT[:], vc[:, :, 0:NSEQ], AF.Silu), reads=['vcols'], writes=['scT'])
        for b in range(NSEQ):
            sel = self.selb[b]
            P.op('dve', lambda e: e.tensor_scalar(sel[:], idf[0:NSEQ, :], 0.0, idf[0:NSEQ, b:b + 1],
                                                  op0=ALU.mult, op1=ALU.add),
                 reads=['ident_f'], writes=[('selb', b)])
        gf = self.Gfin
        fg = d["final_g"]
        P.dma('sp', gf[:], bass.AP(fg.tensor, fg.offset, [[0, 128], [1, D]]), writes=['Gfin'], key='Gfin')
        self.ada_layer(0)

    def ada_layer(self, l):
        P, d, NSEQ = self.P, self.d, self.NSEQ
        scT, idf, arow, g1row, abT = self.scT, self.ident_f, self.arow, self.g1row, self.abT[l]
        ab = d["ada_b"][l]

        def blk(n):
            def fn(ws):
                q = n % 2
                abk = self.abk[q]
                P.dma('sp', abk, bass.AP(ab.tensor, ab.offset + n * 512, [[0, NSEQ], [1, 512]]),
                      writes=[('abk', q)], key=('abk', q))
                bk = self.bank()
                pb = self.banks[bk]
                for half in range(2):
                    buf, keys = ws[half]
                    for kc in range(8):
                        P.op('pe', lambda e: e.matmul(
                            pb[0:NSEQ, half * 256:(half + 1) * 256], scT[:, kc, :], buf[:, kc, 0:256],
                            start=(kc == 0), stop=(kc == 7)),
                            reads=['scT'] + keys, writes=[('psum', bk)])
                P.op('dve', lambda e: e.tensor_tensor(arow[:, q, :], pb[0:NSEQ, :], abk, op=ALU.add),
                     reads=[('psum', bk), ('abk', q)], writes=[('arow', q)])
                if n < 4:
                    bk2 = self.bank()
                    pb2 = self.banks[bk2]
                    for cc in range(4):
                        P.op('pe', lambda e: e.matmul(pb2[:, cc * NSEQ:(cc + 1) * NSEQ],
                                                      arow[0:NSEQ, q, cc * 128:(cc + 1) * 128],
                                                      idf[0:NSEQ, 0:NSEQ], start=True, stop=True),
                             reads=['ident_f', ('arow', q)], writes=[('psum', bk2)])
                    P.op('dve', lambda e: e.tensor_copy(abT[:, 4 * n:4 * n + 4, :].rearrange("p a b -> p (a b)"),
                                                        pb2[:, 0:4 * NSEQ]),
                         reads=[('psum', bk2)], writes=[('abT', l, n)])
                elif l == 0:
                    self.gate_bcast(arow[0:NSEQ, q, :], ('arow', q), n - 4)
                else:
                    P.op('pool', lambda e: e.tensor_copy(g1row[:, (n - 4) * 512:(n - 3) * 512], arow[0:NSEQ, q, :]),
                         reads=[('arow', q)], writes=[('g1row', n - 4)])
            return fn
        items = []
        for n in range(6):
            items.append(([[(d["ada_w"][l][:, n * 512 + h * 256:n * 512 + (h + 1) * 256], 256)] for h in range(2)],
                          blk(n)))
        self.run_items(items)
        A = self.A[l]
        vc = self.vcols
        for b in range(NSEQ):
            P.op('dve', lambda e: e.tensor_scalar(A[:, b, :], abT[:, 8:16, b], 1.0, 1.0, op0=ALU.add, op1=ALU.mult),
                 reads=[('abT', l, 2), ('abT', l, 3)], writes=[('A', l, b)])
            P.op('dve', lambda e: e.tensor_tensor(A[:, b, :], A[:, b, :], vc[:, :, 2 + l], op=ALU.mult),
                 reads=[('A', l, b), 'vcols'], writes=[('A', l, b)])

    def gate_bcast(self, rowsrc, rowkey, n):
        P, NSEQ = self.P, self.NSEQ
        for b in range(NSEQ):
            Gb, sel = self.Gb[b], self.selb[b]
            bk = self.bank()
            pb = self.banks[bk]
            P.op('pe', lambda e: e.matmul(pb[:, :], sel[0:NSEQ, :], rowsrc, start=True, stop=True),
                 reads=[('selb', b), rowkey], writes=[('psum', bk)])
            P.op('act', lambda e: e.activation(Gb[:, n * 512:(n + 1) * 512], pb[:, :], AF.Copy),
                 reads=[('psum', bk)], writes=[('Gb', b, n)])

    def frontend(self, l, b, xsrc, xkey):
        P, NT = self.P, self.NT
        idf, hT, junk = self.ident_f, self.hT, self.junk
        A, abT = self.A[l], self.abT[l]
        bks = {}

        def load(i):
            s = i % 2
            P.dma('sp', self.xt[s][:], xsrc[i * 128:(i + 1) * 128, :], reads=[(xkey, b, i)], writes=[('xt', s)],
                  key=('xt', s))

        def stats(i):
            s = i % 2
            xt, ss, rstd, diag = self.xt[s], self.ss[s], self.rstd[s], self.diag[s]
            P.op('act', lambda e: e.activation(junk[:], xt[:], AF.Square, scale=1.0 / 32, accum_out=ss[:, 0:1]),
                 reads=[('xt', s)], writes=['junk', ('ss', s)])
            P.op('act', lambda e: e.activation(rstd[:], ss[:, 0:1], AF.Ln, bias=EPS),
                 reads=[('ss', s)], writes=[('rstd', s)])
            P.op('act', lambda e: e.activation(rstd[:], rstd[:], AF.Exp, scale=-0.5),
                 reads=[('rstd', s)], writes=[('rstd', s)])
            P.op('dve', lambda e: e.tensor_scalar(diag[:], idf[:], rstd[:, 0:1], 1.0, op0=ALU.mult, op1=ALU.mult),
                 reads=[('rstd', s), 'ident_f'], writes=[('diag', s)])

        def trans(i):
            s = i % 2
            xt, diag = self.xt[s], self.diag[s]
            bks[i] = [self.bank(), self.bank()]
            for c in range(8):
                bk = bks[i][c // 4]
                pb = self.banks[bk]
                P.op('pe', lambda e: e.matmul(pb[:, (c % 4) * 128:(c % 4 + 1) * 128], xt[:, c * 128:(c + 1) * 128],
                                              diag[:], start=True, stop=True),
                     reads=[('xt', s), ('diag', s)], writes=[('psum', bk)])

        def evac(i):
            for c in range(8):
                bk = bks[i][c // 4]
                pb = self.banks[bk]
                src = pb[:, (c % 4) * 128:(c % 4 + 1) * 128]
                dst = hT[:, c, i * 128:(i + 1) * 128]
                rd = [('psum', bk), ('A', l, b), ('abT', l, 0), ('abT', l, 1)]
                wr = [('hT', c, i)]
                if c % 4 == 3:
                    P.op('act', lambda e: e.activation(dst, src, AF.Identity, scale=A[:, b, c:c + 1],
                                                       bias=abT[:, c, b:b + 1]), reads=rd, writes=wr)
                else:
                    P.op('dve', lambda e: e.tensor_scalar(dst, src, A[:, b, c:c + 1], abT[:, c, b:b + 1],
                                                          op0=ALU.mult, op1=ALU.add), reads=rd, writes=wr)
        load(0)
        if NT > 1:
            load(1)
        stats(0)
        trans(0)
        for i in range(NT):
            if i + 1 < NT:
                stats(i + 1)
            if i + 2 < NT:
                load(i + 2)
            if i + 1 < NT:
                trans(i + 1)
            evac(i)

    def hT_keys(self, t0, t1):
        return [('hT', c, i) for c in range(8) for i in range(t0, t1)]

    def load_ow(self, src):
        v = src.rearrange("(j p) f -> p j f", p=128)
        for h in range(2):
            self.P.dma('pool', self.ow[:, 4 * h:4 * h + 4, :], v[:, 4 * h:4 * h + 4, :], writes=[('ow', h)],
                       key=('ow', h))
        return self.ow, [('ow', 0), ('ow', 1)]

    def outproj(self, l, b, ow, owkeys, xsrc, xkey_r, dst, xkey_w, final):
        P, NT = self.P, self.NT
        yT, Gb, junk, gf = self.yT, self.Gb[b], self.junk, self.Gfin

        def load(i):
            s = i % 2
            P.dma('sp', self.xt[s][:], xsrc[i * 128:(i + 1) * 128, :], reads=[(xkey_r, b, i)], writes=[('xt', s)],
                  key=('xt', s))
        load(0)
        for i in range(NT):
            if i + 1 < NT:
                load(i + 1)
            s = i % 2
            xt = self.xt[s]
            for dh in range(2):
                bk = self.bank()
                pb = self.banks[bk]
                for j in range(8):
                    P.op('pe', lambda e, pb=pb, j=j, dh=dh: e.matmul(
                        pb[:, :], yT[:, j, i * 128:(i + 1) * 128], ow[:, j, dh * 512:(dh + 1) * 512],
                        start=(j == 0), stop=(j == 7)),
                        reads=[('yT', j, i), owkeys[j // 4]], writes=[('psum', bk)])
                tt = self.tt[dh]
                P.op('dve', lambda e, pb=pb, tt=tt, dh=dh: e.tensor_tensor(
                    tt[:], pb[:, :], Gb[:, dh * 512:(dh + 1) * 512], op=ALU.mult),
                    reads=[('psum', bk), ('Gb', b, dh)], writes=[('tt', dh)])
                P.op('dve', lambda e, xt=xt, tt=tt, dh=dh: e.tensor_tensor(
                    xt[:, dh * 512:(dh + 1) * 512], tt[:], xt[:, dh * 512:(dh + 1) * 512], op=ALU.add),
                    reads=[('tt', dh), ('xt', s)], writes=[('xt', s)])
            if final:
                ss, rstd = self.ss[s], self.rstd[s]
                P.op('act', lambda e, xt=xt, ss=ss: e.activation(junk[:], xt[:], AF.Square, scale=1.0 / 32,
                                                                accum_out=ss[:, 0:1]),
                     reads=[('xt', s)], writes=['junk', ('ss', s)])
                P.op('act', lambda e, ss=ss, rstd=rstd: e.activation(rstd[:], ss[:, 0:1], AF.Ln, bias=EPS),
                     reads=[('ss', s)], writes=[('rstd', s)])
                P.op('act', lambda e, rstd=rstd: e.activation(rstd[:], rstd[:], AF.Exp, scale=-0.5),
                     reads=[('rstd', s)], writes=[('rstd', s)])
                P.op('dve', lambda e, xt=xt, rstd=rstd: e.scalar_tensor_tensor(
                    xt[:], xt[:], rstd[:, 0:1], gf[:], op0=ALU.mult, op1=ALU.mult),
                    reads=[('xt', s), ('rstd', s), 'Gfin'], writes=[('xt', s)])
            P.dma('sp', dst[i * 128:(i + 1) * 128, :], xt[:], reads=[('xt', s)], writes=[(xkey_w, b, i)],
                  key=('xst', s))

    def even_consts(self):
        P, d = self.P, self.d
        cv = self.carve
        self.wTm = cv([8, 128], BF16)
        self.bB = cv([8 * 128], F32)
        self.poolw = cv([4, 2, 256], BF16)
        self.invc = cv([4, 16], F32)
        self.iot = cv([16], F32)
        self.aT = [cv([32 + 512], F32) for i in range(2)]
        self.t1 = cv([528], F32)
        self.t2 = cv([528], F32)
        self.t16 = cv([16], F32)
        self.t1b = cv([528], F32)
        self.t2b = cv([528], F32)
        self.sg4 = [cv([512], F32) for i in range(4)]
        self.pT = [[cv([512], BF16) for k in range(2)] for q in range(2)]
        self.vn = [cv([D], BF16) for i in range(2)]
        self.wT_f = cv([8, 128], F32)
        wTf, wTm, bB, poolw, invc, iot = self.wT_f, self.wTm, self.bB, self.poolw, self.invc, self.iot
        P.dma('sp', wTf, d["sgu_wT"].rearrange("h s t -> s h t"), writes=['wT_f'], key='wT_f')
        for h in range(8):
            P.op('pool', lambda e: e.affine_select(wTf[:, h, :], wTf[:, h, :], pattern=[[1, 128]],
                                                  compare_op=ALU.is_ge, fill=0.0, base=0, channel_multiplier=-1),
                 reads=['wT_f'], writes=['wT_f'])
        P.op('pool', lambda e: e.tensor_copy(wTm, wTf), reads=['wT_f'], writes=['wTm'])
        sbv = d["sgu_b"]
        P.dma('sp', bB, bass.AP(sbv.tensor, sbv.offset, [[0, 128], [1, 1024]]), writes=['bB'], key='bB')
        P.dma('pool', poolw, d["pool_w"].rearrange("g (k p) d -> p g k d", p=128), writes=['poolw'], key='poolw')
        P.op('pool', lambda e: e.iota(iot, pattern=[[1, 16]], base=1, channel_multiplier=0,
                                      allow_small_or_imprecise_dtypes=True), writes=['iot'])
        for g, w in enumerate(POOL_WINDOWS):
            P.op('dve', lambda e: e.tensor_scalar(invc[:, g, :], iot, float(w), 1.0, op0=ALU.min, op1=ALU.mult),
                 reads=['iot'], writes=[('invc', g)])
            P.op('dve', lambda e: e.reciprocal(invc[:, g, :], invc[:, g, :]), reads=[('invc', g)],
                 writes=[('invc', g)])

    def even_half1(self, b, hook):
        P, d, NG = self.P, self.d, self.NG
        hT, yT, vc = self.hT, self.yT, self.vcols
        W = d["even_in_w"]
        t1, t2, t16, invc, poolw = self.t1, self.t2, self.t16, self.invc, self.poolw
        banks = self.banks
        work = []

        def S1(u):
            g, tg, q, wa, ka = u['g'], u['tg'], u['q'], u['wa'], u['ka']
            w = POOL_WINDOWS[g]
            hk = self.hT_keys(4 * tg, 4 * tg + 4)
            for jj in range(2):
                aT = self.aT[jj]
                eng = 'pool' if jj == 0 else 'dve'
                if tg == 0:
                    P.op('pool', lambda e: e.memset(aT[:, 0:32], 0.0), writes=[('aT', jj)])
                else:
                    P.op(eng, lambda e: e.tensor_copy(aT[:, 0:32], aT[:, 512:544]), reads=[('aT', jj)],
                         writes=[('aT', jj)])
                bk = self.bank()
                pb = banks[bk]
                for kc in range(8):
                    P.op('pe', lambda e: e.matmul(pb[:, :], wa[:, kc, jj * 128:(jj + 1) * 128],
                                                  hT[:, kc, tg * 512:(tg + 1) * 512], start=(kc == 0), stop=(kc == 7)),
                         reads=ka + hk, writes=[('psum', bk)])
                P.op('act', lambda e: e.activation(aT[:, 32:544], pb[:, :], AF.Copy),
                     reads=[('psum', bk), ('aT', jj)], writes=[('aT', jj)])
                ta, tb = (t1, t2) if jj == 0 else (self.t1b, self.t2b)
                ka_, kb_ = ('t1', jj), ('t2', jj)
                P.op(eng, lambda e: e.tensor_tensor(ta[:, 0:528], aT[:, 16:544], aT[:, 15:543], op=ALU.add),
                     reads=[('aT', jj)], writes=[ka_])
                cur, ck = ta, ka_
                if w >= 4:
                    P.op(eng, lambda e: e.tensor_tensor(tb[:, 2:528], ta[:, 2:528], ta[:, 0:526], op=ALU.add),
                         reads=[ka_], writes=[kb_])
                    cur, ck = tb, kb_
                if w >= 8:
                    P.op(eng, lambda e: e.tensor_tensor(ta[:, 6:528], tb[:, 6:528], tb[:, 2:524], op=ALU.add),
                         reads=[kb_], writes=[ka_])
                    cur, ck = ta, ka_
                if w >= 16:
                    P.op(eng, lambda e: e.tensor_tensor(tb[:, 14:528], ta[:, 14:528], ta[:, 6:520], op=ALU.add),
                         reads=[ka_], writes=[kb_])
                    cur, ck = tb, kb_
                pT = self.pT[q][jj]
                P.op('dve', lambda e: e.scalar_tensor_tensor(pT[:, :], cur[:, 16:528], 1.0 / w, aT[:, 32:544],
                                                             op0=ALU.mult, op1=ALU.subtract),
                     reads=[ck, ('aT', jj)], writes=[('pT', q, jj)])
                if tg == 0:
                    P.op('dve', lambda e: e.tensor_tensor(t16[:], cur[:, 16:32], invc[:, g, :], op=ALU.mult),
                         reads=[ck, ('invc', g)], writes=['t16'])
                    P.op('dve', lambda e: e.tensor_tensor(pT[:, 0:16], t16[:], aT[:, 32:48], op=ALU.subtract),
                         reads=['t16', ('aT', jj), ('pT', q, jj)], writes=[('pT', q, jj)])

        def S2(u):
            g, tg, wg, kg = u['g'], u['tg'], u['wg'], u['kg']
            hk = self.hT_keys(4 * tg, 4 * tg + 4)
            for jo in range(2):
                bkg = self.bank()
                pbg = banks[bkg]
                for kc in range(8):
                    P.op('pe', lambda e: e.matmul(pbg[:, :], wg[:, kc, jo * 128:(jo + 1) * 128],
                                                  hT[:, kc, tg * 512:(tg + 1) * 512], start=(kc == 0), stop=(kc == 7)),
                         reads=kg + hk, writes=[('psum', bkg)])
                sg = self.sg4[2 * u['q'] + jo]
                P.op('act', lambda e: e.activation(sg, pbg[:, :], AF.Silu), reads=[('psum', bkg)],
                     writes=[('sg4', 2 * u['q'] + jo)])

        def S3(u):
            g, tg, q = u['g'], u['tg'], u['q']
            for jo in range(2):
                j = 2 * g + jo
                bkp = self.bank()
                pbp = banks[bkp]
                for kc in range(2):
                    pT = self.pT[q][kc]
                    P.op('pe', lambda e: e.matmul(pbp[:, :], poolw[:, g, kc, jo * 128:(jo + 1) * 128], pT[:, :],
                                                  start=(kc == 0), stop=(kc == 1)),
                         reads=['poolw', ('pT', q, kc)], writes=[('psum', bkp)])
                sg = self.sg4[2 * q + jo]
                P.op('dve', lambda e: e.scalar_tensor_tensor(yT[:, j, tg * 512:(tg + 1) * 512], pbp[:, :],
                                                             vc[:, j, 4:5], sg, op0=ALU.mult, op1=ALU.mult),
                     reads=[('psum', bkp), ('sg4', 2 * q + jo), 'vcols'],
                     writes=[('yT', j, i) for i in range(4 * tg, 4 * tg + 4)])

        specs = [[[(W[:, 256 * g:256 * g + 256], 256)], [(W[:, 3072 + 256 * g:3072 + 256 * g + 256], 256)]]
                 for g in range(4)]
        ids = {}

        def load(g):
            self.item_ctr += 1
            self.cur_issue = ids[g] = self.item_ctr
            return [self.load_w(parts) for parts in specs[g]]
        ws = {0: load(0)}
        units = []
        for g in range(4):
            for tg in range(NG):
                units.append(dict(g=g, tg=tg, q=len(units) % 2))
        n = len(units)

        def bind(u):
            (wa, ka), (wg, kg) = ws[u['g']]
            u.update(wa=wa, ka=ka, wg=wg, kg=kg)
        bind(units[0])
        S1(units[0])
        for k, u in enumerate(units):
            if u['tg'] == 0 and u['g'] + 1 < 4:
                ws[u['g'] + 1] = load(u['g'] + 1)
            if k == 2 and hook is not None:
                hook()
            S2(u)
            if k + 1 < n:
                bind(units[k + 1])
                S1(units[k + 1])
            S3(u)
            if u['tg'] == NG - 1:
                self.item_done.add(ids[u['g']])
        if n <= 2 and hook is not None:
            hook()

    def even_half2(self, b, hook):
        P, d, NG, NT = self.P, self.d, self.NG, self.NT
        hT, yT, vc, junk = self.hT, self.yT, self.vcols, self.junk
        wTm, bB = self.wTm, self.bB
        W = d["even_in_w"]

        def vitem(ws):
            st = {}

            def V1(i):
                s = i % 2
                hk = self.hT_keys(i, i + 1)
                ss = self.ss[s]
                bks = [self.bank(), self.bank()]
                st[i] = bks
                for hf in range(2):
                    pb = self.banks[bks[hf]]
                    for q in range(2):
                        wv, kv = ws[2 * hf + q]
                        for kc in range(8):
                            P.op('pe', lambda e: e.matmul(
                                pb[:, q * 256:(q + 1) * 256], hT[:, kc, i * 128:(i + 1) * 128], wv[:, kc, 0:256],
                                start=(kc == 0), stop=(kc == 7)), reads=kv + hk, writes=[('psum', bks[hf])])

            def V1b(i):
                s = i % 2
                ss = self.ss[s]
                bks = st[i]
                for hf in range(2):
                    pb = self.banks[bks[hf]]
                    P.op('act', lambda e: e.activation(junk[:, 0:512], pb[:, :], AF.Square,
                                                       accum_out=ss[:, hf:hf + 1]),
                         reads=[('psum', bks[hf])], writes=['junk', ('ss', s)])

            def V2(i):
                s = i % 2
                ss, rstd, vn = self.ss[s], self.rstd[s], self.vn[s]
                bks = st[i]
                P.op('dve', lambda e: e.tensor_tensor(ss[:, 0:1], ss[:, 0:1], ss[:, 1:2], op=ALU.add),
                     reads=[('ss', s)], writes=[('ss', s)])
                P.op('act', lambda e: e.activation(rstd[:], ss[:, 0:1], AF.Ln, scale=1.0 / 1024, bias=EPS),
                     reads=[('ss', s)], writes=[('rstd', s)])
                P.op('act', lambda e: e.activation(rstd[:], rstd[:], AF.Exp, scale=-0.5),
                     reads=[('rstd', s)], writes=[('rstd', s)])
                pb0, pb1 = self.banks[bks[0]], self.banks[bks[1]]
                P.op('dve', lambda e: e.tensor_scalar(
                    vn[:, 0:512], pb0[:, :], rstd[:, 0:1], 1.0, op0=ALU.mult, op1=ALU.mult),
                    reads=[('psum', bks[0]), ('rstd', s)], writes=[('vn', s, 0)])
                P.op('act', lambda e: e.activation(vn[:, 512:1024], pb1[:, :], AF.Copy, scale=rstd[:, 0:1]),
                     reads=[('psum', bks[1]), ('rstd', s)], writes=[('vn', s, 1)])

            def V3(i):
                s = i % 2
                vn = self.vn[s]
                zb = [self.bank(), self.bank()]
                for hh in range(8):
                    pz = self.banks[zb[hh // 4]]
                    P.op('pe', lambda e: e.matmul(
                        pz[:, (hh % 4) * 128:(hh % 4 + 1) * 128], vn[:, hh * 128:(hh + 1) * 128], wTm[:, hh, :],
                        start=True, stop=True),
                        reads=[('vn', s, hh // 4), 'wTm'], writes=[('psum', zb[hh // 4])])
                for hh in range(8):
                    pz = self.banks[zb[hh // 4]]
                    P.op('dve', lambda e: e.scalar_tensor_tensor(
                        yT[:, hh, i * 128:(i + 1) * 128], pz[:, (hh % 4) * 128:(hh % 4 + 1) * 128], vc[:, hh, 5:6],
                        bB[:, hh * 128:(hh + 1) * 128], op0=ALU.mult, op1=ALU.add),
                        reads=[('psum', zb[hh // 4]), 'vcols', 'bB'], writes=[('yT', hh, i)])
            V1(0)
            V1b(0)
            for i in range(NT):
                if i + 1 < NT:
                    V1(i + 1)
                V2(i)
                if i + 1 < NT:
                    V1b(i + 1)
                V3(i)

        def uitem(hp):
            def fn(ws):
                (wu, ku), (wg, kg) = ws
                for tg in range(NG):
                    hk = self.hT_keys(4 * tg, 4 * tg + 4)
                    for jo in range(2):
                        hh = 2 * hp + jo
                        bku, bkg = self.bank(), self.bank()
                        pbu, pbg = self.banks[bku], self.banks[bkg]
                        for kc in range(8):
                            P.op('pe', lambda e: e.matmul(
                                pbu[:, :], wu[:, kc, jo * 128:(jo + 1) * 128], hT[:, kc, tg * 512:(tg + 1) * 512],
                                start=(kc == 0), stop=(kc == 7)), reads=ku + hk, writes=[('psum', bku)])
                        for kc in range(8):
                            P.op('pe', lambda e: e.matmul(
                                pbg[:, :], wg[:, kc, jo * 128:(jo + 1) * 128], hT[:, kc, tg * 512:(tg + 1) * 512],
                                start=(kc == 0), stop=(kc == 7)), reads=kg + hk, writes=[('psum', bkg)])
                        sg, tt = self.sg[jo], self.tt[jo]
                        P.op('act', lambda e: e.activation(sg[:], pbg[:, :], AF.Silu),
                             reads=[('psum', bkg)], writes=[('sg', jo)])
                        P.op('dve', lambda e: e.tensor_tensor(tt[:], pbu[:, :], sg[:], op=ALU.mult),
                             reads=[('psum', bku), ('sg', jo)], writes=[('tt', jo)])
                        yk = [('yT', hh, i) for i in range(4 * tg, 4 * tg + 4)]
                        P.op('pool', lambda e: e.tensor_tensor(
                            yT[:, hh, tg * 512:(tg + 1) * 512], tt[:], yT[:, hh, tg * 512:(tg + 1) * 512], op=ALU.mult),
                            reads=[('tt', jo)] + yk, writes=yk)
            return fn
        items = [([[(W[:, 2048 + 256 * q:2048 + 256 * q + 256], 256)] for q in range(4)], vitem)]
        for hp in range(4):
            items.append(([[(W[:, 1024 + 256 * hp:1024 + 256 * hp + 256], 256)],
                           [(W[:, 4096 + 256 * hp:4096 + 256 * hp + 256], 256)]], uitem(hp)))
        self.run_items(items, hook=hook)

    def odd_consts(self):
        P, d, S, NT = self.P, self.d, self.S, self.NT
        self.ar_off = 0
        cv = self.carve
        self.M1f = cv([128], F32); self.M2f = cv([128], F32)
        self.negTri = cv([128], BF16); self.negOnes = cv([128], BF16); self.ones_b = cv([128], BF16)
        self.NEGm = cv([4, 512], BF16)
        self.gw = cv([512], F32, part=16); self.gbrow = cv([512], F32, part=1); self.ones1 = cv([128], F32, part=1)
        self.wglr = cv([8, 16], BF16)
        self.qdT = cv([S], BF16); self.kiT = cv([S], BF16)
        self.kend = cv([NT, 128], BF16); self.gv = cv([NT, 256], BF16)
        self.f32 = [cv([512], F32) for _ in range(5)]
        self.b16 = [cv([512], BF16) for _ in range(7)]
        self.S32 = cv([256], F32); self.dec = cv([NT], F32); self.glrT = cv([512], F32, part=16)
        self.qT2 = cv([S], BF16); self.kT2 = cv([S], BF16); self.vv1 = cv([NT, 128], BF16)
        M1f, M2f, negTri, negOnes, ones_b, NEGm = self.M1f, self.M2f, self.negTri, self.negOnes, self.ones_b, self.NEGm
        t0, t1 = self.f32[0], self.f32[1]
        P.op('pool', lambda e: e.memset(M1f, 1.0), writes=['M1f'])
        P.op('pool', lambda e: e.affine_select(M1f, M1f, pattern=[[1, 128]], compare_op=ALU.is_ge, fill=0.0, base=0,
                                               channel_multiplier=-1), reads=['M1f'], writes=['M1f'])
        P.op('pool', lambda e: e.memset(M2f, 1.0), writes=['M2f'])
        P.op('pool', lambda e: e.affine_select(M2f, M2f, pattern=[[-1, 128]], compare_op=ALU.is_gt, fill=0.0, base=0,
                                               channel_multiplier=1), reads=['M2f'], writes=['M2f'])
        P.op('pool', lambda e: e.memset(t1[:, 0:128], -1.0), writes=[('f32', 1)])
        P.op('pool', lambda e: e.affine_select(t1[:, 0:128], t1[:, 0:128], pattern=[[-1, 128]], compare_op=ALU.is_ge,
                                               fill=0.0, base=0, channel_multiplier=1),
             reads=[('f32', 1)], writes=[('f32', 1)])
        P.op('pool', lambda e: e.tensor_copy(negTri, t1[:, 0:128]), reads=[('f32', 1)], writes=['negTri'])
        P.op('pool', lambda e: e.memset(t1[:, 0:128], -1.0), reads=[('f32', 1)], writes=[('f32', 1)])
        P.op('pool', lambda e: e.tensor_copy(negOnes, t1[:, 0:128]), reads=[('f32', 1)], writes=['negOnes'])
        P.op('pool', lambda e: e.memset(t1[:, 0:128], 1.0), reads=[('f32', 1)], writes=[('f32', 1)])
        P.op('pool', lambda e: e.tensor_copy(ones_b, t1[:, 0:128]), reads=[('f32', 1)], writes=['ones_b'])
        for a in range(4):
            P.op('pool', lambda e: e.memset(t0, 0.0), reads=[('f32', 0)], writes=[('f32', 0)])
            P.op('pool', lambda e: e.affine_select(t0, t0, pattern=[[1, 512]], compare_op=ALU.is_gt, fill=NEG,
                                                   base=-a * 128, channel_multiplier=-1),
                 reads=[('f32', 0)], writes=[('f32', 0)])
            P.op('pool', lambda e: e.tensor_copy(NEGm[:, a, :], t0), reads=[('f32', 0)], writes=[('NEGm', a)])
        P.dma('sp', self.gw, d["gla_gate_w"], writes=['gw'], key='gw')
        P.dma('sp', self.gbrow, d["gla_gate_b"].rearrange("(o n) -> o n", o=1), writes=['gbrow'], key='gbrow')
        P.op('pool', lambda e: e.memset(self.ones1, 1.0), writes=['ones1'])
        P.dma('pool', self.wglr, d["odd_in_w"][:, 2048:2064].rearrange("(k p) f -> p k f", p=128), writes=['wglr'],
              key='wglr')

    def odd_half1(self, b, hook):
        P, d, NG, NT = self.P, self.d, self.NG, self.NT
        hT, yT, vc = self.hT, self.yT, self.vcols
        W = d["odd_in_w"]
        M1f, M2f, ones_b, gw, gbrow, ones1, wglr = self.M1f, self.M2f, self.ones_b, self.gw, self.gbrow, self.ones1, self.wglr
        qdT, kiT, kend, gv, S32, dec, glrT = self.qdT, self.kiT, self.kend, self.gv, self.S32, self.dec, self.glrT
        Ed, Ei, Es, sp4, rsb = self.f32
        osq = [self.b16[0], self.b16[1]]
        attm = [self.b16[3][:, 0:128], self.b16[3][:, 128:256]]
        Sb = [self.b16[5][:, 0:256], self.b16[5][:, 256:512]]
        banks = self.banks
        QS = 128 ** -0.5

        e1 = self.junk[:, :].bitcast(F32)

        def step1_pieces(h, tg, wqk, kqk, wv, kv):
            hk = self.hT_keys(4 * tg, 4 * tg + 4)
            st = {}

            def p_glr():
                bk = self.bank(); pb = banks[bk]
                for kc in range(8):
                    P.op('pe', lambda e: e.matmul(pb[0:16, :], wglr[:, kc, :], hT[:, kc, tg * 512:(tg + 1) * 512],
                                                  start=(kc == 0), stop=(kc == 7)), reads=['wglr'] + hk,
                         writes=[('psum', bk)])
                P.op('act', lambda e: e.activation(glrT, pb[0:16, :], AF.Copy), reads=[('psum', bk)], writes=['glrT'])

            def p_la():
                bk = self.bank(); pbL = banks[bk]
                for ti in range(4):
                    P.op('pe', lambda e: e.matmul(pbL[:, ti * 128:(ti + 1) * 128], glrT[0:16, ti * 128:(ti + 1) * 128],
                                                  gw[0:16, h * 128:(h + 1) * 128], start=True, stop=False),
                         reads=['glrT', 'gw'], writes=[('psum', bk)])
                    P.op('pe', lambda e: e.matmul(pbL[:, ti * 128:(ti + 1) * 128], ones1[0:1, :],
                                                  gbrow[0:1, h * 128:(h + 1) * 128], start=False, stop=True),
                         reads=['ones1', 'gbrow'], writes=[('psum', bk)])
                P.op('act', lambda e: e.activation(e1, pbL[:, :], AF.Exp, scale=-1.0), reads=[('psum', bk)],
                     writes=['junk'])
                P.op('act', lambda e: e.activation(sp4, e1, AF.Ln, bias=1.0), reads=['junk'], writes=[('f32', 3)])

            def p_bd():
                bkB = self.bank(); pbB = banks[bkB]
                for ti in range(4):
                    P.op('pe', lambda e: e.matmul(pbB[:, ti * 128:(ti + 1) * 128], sp4[:, ti * 128:(ti + 1) * 128], M1f,
                                                  start=True, stop=True), reads=[('f32', 3), 'M1f'],
                         writes=[('psum', bkB)])
                bkD = self.bank(); pbD = banks[bkD]
                for ti in range(4):
                    P.op('pe', lambda e: e.matmul(pbD[:, ti * 128:(ti + 1) * 128], M2f, sp4[:, ti * 128:(ti + 1) * 128],
                                                  start=True, stop=True), reads=[('f32', 3), 'M2f'],
                         writes=[('psum', bkD)])
                P.op('act', lambda e: e.activation(Ed, pbB[:, :], AF.Exp, scale=-1.0 / 16), reads=[('psum', bkB)],
                     writes=[('f32', 0)])
                P.op('act', lambda e: e.activation(Ei, pbB[:, :], AF.Exp, scale=1.0 / 16), reads=[('psum', bkB)],
                     writes=[('f32', 1)])
                P.op('act', lambda e: e.activation(Es, pbD[:, :], AF.Exp, scale=-1.0 / 16), reads=[('psum', bkD)],
                     writes=[('f32', 2)])
                P.op('pool', lambda e: e.tensor_copy(dec[:, 4 * tg:4 * tg + 4],
                                                     Ed.rearrange("p (a b) -> p a b", a=4)[:, :, 127]),
                     reads=[('f32', 0)], writes=[('dec', tg)])

            def p_q():
                bkQ = self.bank(); pbQ = banks[bkQ]
                for kc in range(8):
                    P.op('pe', lambda e: e.matmul(pbQ[:, :], wqk[:, kc, 0:128], hT[:, kc, tg * 512:(tg + 1) * 512],
                                                  start=(kc == 0), stop=(kc == 7)), reads=kqk + hk,
                         writes=[('psum', bkQ)])
                P.op('dve', lambda e: e.scalar_tensor_tensor(qdT[:, tg * 512:(tg + 1) * 512], pbQ[:, :], QS, Ed,
                                                             op0=ALU.mult, op1=ALU.mult),
                     reads=[('psum', bkQ), ('f32', 0)], writes=[('qdT', tg)])

            def p_k():
                bkK = self.bank(); pbK = banks[bkK]
                for kc in range(8):
                    P.op('pe', lambda e: e.matmul(pbK[:, :], wqk[:, kc, 128:256], hT[:, kc, tg * 512:(tg + 1) * 512],
                                                  start=(kc == 0), stop=(kc == 7)), reads=kqk + hk,
                         writes=[('psum', bkK)])
                P.op('dve', lambda e: e.tensor_tensor(kiT[:, tg * 512:(tg + 1) * 512], pbK[:, :], Ei, op=ALU.mult),
                     reads=[('psum', bkK), ('f32', 1)], writes=[('kiT', tg)])

            def p_t():
                bkT = self.bank(); pbT = banks[bkT]
                for ti in range(4):
                    i = 4 * tg + ti
                    for kc in range(8):
                        P.op('pe', lambda e: e.matmul(pbT[:, ti * 128:(ti + 1) * 128], hT[:, kc, i * 128:(i + 1) * 128],
                                                      wqk[:, kc, 128:256], start=(kc == 0), stop=(kc == 7)),
                             reads=kqk + hk, writes=[('psum', bkT)])
                P.op('dve', lambda e: e.tensor_tensor(kend[:, 4 * tg:4 * tg + 4, :].rearrange("p a b -> p (a b)"),
                                                      pbT[:, :], Es, op=ALU.mult),
                     reads=[('psum', bkT), ('f32', 2)], writes=[('kend', tg)])

            def p_v(hv):
                def f():
                    bkV = self.bank(); pbV = banks[bkV]
                    for t2 in range(2):
                        i = 4 * tg + 2 * hv + t2
                        for kc in range(8):
                            P.op('pe', lambda e: e.matmul(pbV[:, t2 * 256:(t2 + 1) * 256],
                                                          hT[:, kc, i * 128:(i + 1) * 128], wv[:, kc, 0:256],
                                                          start=(kc == 0), stop=(kc == 7)),
                                 reads=kv + hk, writes=[('psum', bkV)])
                    dst = gv[:, 4 * tg + 2 * hv:4 * tg + 2 * hv + 2, :].rearrange("p a b -> p (a b)")
                    P.op('dve', lambda e: e.tensor_copy(dst, pbV[:, :]), reads=[('psum', bkV)],
                         writes=[('gv', tg, hv)])
                return f
            return [p_glr, p_v(0), p_la, p_v(1), p_bd, p_q, p_k, p_t]

        def step23(h, tg, wg, kg, pieces):
            hk = self.hT_keys(4 * tg, 4 * tg + 4)
            O = [banks[6], banks[7]]
            for v2 in range(2):
                bkg = self.bank(); pbg = banks[bkg]
                for kc in range(8):
                    P.op('pe', lambda e: e.matmul(pbg[:, :], wg[:, kc, v2 * 128:(v2 + 1) * 128],
                                                  hT[:, kc, tg * 512:(tg + 1) * 512], start=(kc == 0), stop=(kc == 7)),
                         reads=kg + hk, writes=[('psum', bkg)])
                sg = self.sg[v2]
                self.silu_gate(pbg[:, :], ('psum', bkg), sg[:], ('sg', v2))
                P.op('dve', lambda e: e.tensor_tensor(sg[:], pbg[:, :], sg[:], op=ALU.mult),
                     reads=[('psum', bkg), ('sg', v2)], writes=[('sg', v2)])
            for ti in range(4):
                for _ in range(2 if ti < 2 else 1):
                    if pieces:
                        pieces.pop(0)()
                i = 4 * tg + ti
                q = i % 2
                bkA = self.bank(); pbA = banks[bkA]
                P.op('pe', lambda e: e.matmul(pbA[:, 0:128], kiT[:, i * 128:(i + 1) * 128], qdT[:, i * 128:(i + 1) * 128],
                                              start=True, stop=True), reads=[('kiT', tg), ('qdT', tg)],
                     writes=[('psum', bkA)])
                P.op('dve', lambda e: e.tensor_tensor(attm[q], pbA[:, 0:128], M1f, op=ALU.mult),
                     reads=[('psum', bkA), 'M1f'], writes=[('attm', q)])
                for v2 in range(2):
                    P.op('pe', lambda e: e.matmul(O[v2][:, ti * 128:(ti + 1) * 128], gv[:, i, v2 * 128:(v2 + 1) * 128],
                                                  attm[q], start=True, stop=False),
                         reads=[('gv', tg, ti // 2), ('attm', q)], writes=[('psum', 6 + v2)])
                    P.op('pe', lambda e: e.matmul(O[v2][:, ti * 128:(ti + 1) * 128], Sb[q][:, v2 * 128:(v2 + 1) * 128],
                                                  qdT[:, i * 128:(i + 1) * 128], start=False, stop=True),
                         reads=[('Sb', q), ('qdT', tg)], writes=[('psum', 6 + v2)])
                bkU = self.bank(); pbU = banks[bkU]
                P.op('pe', lambda e: e.matmul(pbU[:, 0:256], kend[:, i, :], gv[:, i, :], start=True, stop=True),
                     reads=[('kend', tg), ('gv', tg, ti // 2)], writes=[('psum', bkU)])
                P.op('dve', lambda e: e.scalar_tensor_tensor(S32, S32, dec[:, i:i + 1], pbU[:, 0:256],
                                                             op0=ALU.mult, op1=ALU.add),
                     reads=['S32', ('dec', tg), ('psum', bkU)], writes=['S32'])
                P.op('act', lambda e: e.activation(Sb[1 - q], S32, AF.Copy), reads=['S32'], writes=[('Sb', 1 - q)])
            while len(pieces) > 2:
                pieces.pop(0)()
            for v2 in range(2):
                P.op('act', lambda e: e.activation(osq[v2], O[v2][:, :], AF.Square), reads=[('psum', 6 + v2)],
                     writes=[('b16', v2)])
            if pieces:
                pieces.pop(0)()
            bkN = self.bank(); pbN = banks[bkN]
            for v2 in range(2):
                P.op('pe', lambda e: e.matmul(pbN[:, :], ones_b, osq[v2], start=(v2 == 0), stop=(v2 == 1)),
                     reads=['ones_b', ('b16', v2)], writes=[('psum', bkN)])
            P.op('act', lambda e: e.activation(rsb, pbN[:, :], AF.Ln, scale=1.0 / 256, bias=EPS), reads=[('psum', bkN)],
                 writes=[('f32', 4)])
            P.op('act', lambda e: e.activation(rsb, rsb, AF.Exp, scale=-0.5), reads=[('f32', 4)], writes=[('f32', 4)])
            while pieces:
                pieces.pop(0)()
            for v2 in range(2):
                j = 2 * h + v2
                sg, tt = self.sg[v2], self.tt[v2]
                P.op('dve', lambda e: e.scalar_tensor_tensor(tt[:], O[v2][:, :], vc[:, j, 6:7], rsb,
                                                             op0=ALU.mult, op1=ALU.mult),
                     reads=[('psum', 6 + v2), 'vcols', ('f32', 4)], writes=[('tt', v2)])
                yk = [('yT', j, i) for i in range(4 * tg, 4 * tg + 4)]
                P.op('dve', lambda e: e.tensor_tensor(yT[:, j, tg * 512:(tg + 1) * 512], tt[:], sg[:], op=ALU.mult),
                     reads=[('tt', v2), ('sg', v2)], writes=yk)

        def wspec(h):
            return [[(W[:, 128 * h:128 * h + 128], 128), (W[:, 512 + 128 * h:512 + 128 * h + 128], 128)],
                    [(W[:, 1024 + 256 * h:1024 + 256 * h + 256], 256)],
                    [(W[:, 5136 + 256 * h:5136 + 256 * h + 256], 256)]]
        ids = {}

        def load(h):
            self.item_ctr += 1
            self.cur_issue = ids[h] = self.item_ctr
            return [self.load_w(parts) for parts in wspec(h)]
        ws = {0: load(0)}
        self.nrr = 6
        self.bankrr = 0
        (wqk, kqk), (wv, kv), (wg, kg) = ws[0]
        for pc in step1_pieces(0, 0, wqk, kqk, wv, kv):
            pc()
        for h in range(4):
            (wqk, kqk), (wv, kv), (wg, kg) = ws[h]
            if h + 1 < 4:
                ws[h + 1] = load(h + 1)
            if h == 1 and hook is not None:
                hook()
            P.op('pool', lambda e: e.memset(S32, 0.0), writes=['S32'])
            P.op('pool', lambda e: e.memset(Sb[0], 0.0), writes=[('Sb', 0)])
            if NG == 1 and h > 0:
                for pc in step1_pieces(h, 0, wqk, kqk, wv, kv):
                    pc()
            for tg in range(NG):
                if tg + 1 < NG:
                    nxt = step1_pieces(h, tg + 1, wqk, kqk, wv, kv)
                elif h + 1 < 4 and NG > 1:
                    (wqk2, kqk2), (wv2, kv2), _ = ws[h + 1]
                    nxt = step1_pieces(h + 1, 0, wqk2, kqk2, wv2, kv2)
                else:
                    nxt = []
                step23(h, tg, wg, kg, nxt)
            self.item_done.add(ids[h])
        self.nrr = 8

    def silu_gate(self, pbG, gkey, sgbuf, sgkey):
        P = self.P
        P.op('act', lambda e: e.activation(sgbuf, pbG, AF.Exp, scale=-1.0), reads=[gkey], writes=[sgkey])
        P.op('act', lambda e: e.activation(sgbuf, sgbuf, AF.Ln, bias=1.0), reads=[sgkey], writes=[sgkey])
        P.op('act', lambda e: e.activation(sgbuf, sgbuf, AF.Exp, scale=-1.0), reads=[sgkey], writes=[sgkey])

    def odd_half2(self, b, hook):
        P, d, NG, NT = self.P, self.d, self.NG, self.NT
        hT, yT, idb = self.hT, self.yT, self.ident_b[:, :]
        W = d["odd_in_w"]
        negTri, negOnes, NEGm = self.negTri, self.negOnes, self.NEGm
        qTs = [self.qdT, self.qT2]
        kTs = [self.kiT, self.kT2]
        vvs = [self.gv[:, :, 0:128], self.vv1]
        eb = self.f32[0:3]
        xx = self.f32[3:5]
        Lb = self.b16[0:3]
        wb = self.b16[3:5]
        Rb = self.b16[5:7]
        banks = self.banks
        QS = 128 ** -0.5
        NEGtri = NEGm[:, 0, 0:128]

        def vkeys(p, tg):
            return [('gv', tg, 0), ('gv', tg, 1)] if p == 0 else [('vv1', tg)]

        def step1_pieces(hh, ws, bank_fn):
            (wqk, kqk), (wvg, kvg) = ws
            p = hh % 2
            qT, kT, vv = qTs[p], kTs[p], vvs[p]
            pieces = []
            for tg in range(NG):
                hk = self.hT_keys(4 * tg, 4 * tg + 4)
                st = {}

                def pqk(which, half, tg=tg, hk=hk, st=st):
                    def f():
                        if half == 0:
                            st[which] = bank_fn()
                        bk = st[which]; pb = banks[bk]
                        lo = 0 if which == 'q' else 128
                        for kc in range(4 * half, 4 * half + 4):
                            P.op('pe', lambda e: e.matmul(pb[:, :], wqk[:, kc, lo:lo + 128],
                                                          hT[:, kc, tg * 512:(tg + 1) * 512],
                                                          start=(kc == 0), stop=(kc == 7)), reads=kqk + hk,
                                 writes=[('psum', bk)])
                        if half == 1:
                            if which == 'q':
                                P.op('dve', lambda e: e.tensor_scalar(qT[:, tg * 512:(tg + 1) * 512], pb[:, :], QS, 1.0,
                                                                      op0=ALU.mult, op1=ALU.mult),
                                     reads=[('psum', bk)], writes=[('qT', p, tg)])
                            else:
                                P.op('dve', lambda e: e.tensor_copy(kT[:, tg * 512:(tg + 1) * 512], pb[:, :]),
                                     reads=[('psum', bk)], writes=[('kT', p, tg)])
                    return f

                def pv(half, tg=tg, hk=hk, st=st):
                    def f():
                        if half == 0:
                            st['v'] = bank_fn()
                        bk = st['v']; pb = banks[bk]
                        for ti in range(2 * half, 2 * half + 2):
                            i = 4 * tg + ti
                            for kc in range(8):
                                P.op('pe', lambda e: e.matmul(pb[:, ti * 128:(ti + 1) * 128],
                                                              hT[:, kc, i * 128:(i + 1) * 128], wvg[:, kc, 0:128],
                                                              start=(kc == 0), stop=(kc == 7)),
                                     reads=kvg + hk, writes=[('psum', bk)])
                        if half == 1:
                            P.op('dve', lambda e: e.tensor_copy(vv[:, 4 * tg:4 * tg + 4, :],
                                                                pb[:, :].rearrange("p (a b) -> p a b", a=4)),
                                 reads=[('psum', bk)], writes=vkeys(p, tg))
                    return f
                pieces += [pqk('q', 0), pqk('q', 1), pqk('k', 0), pqk('k', 1), pv(0), pv(1)]
            return pieces

        def step2(hh, ws, pieces):
            (wqk, kqk), (wvg, kvg) = ws
            p = hh % 2
            qT, kT, vv = qTs[p], kTs[p], vvs[p]
            tiles = []
            for qg in range(NG):
                jbs = list(range(4 * qg + 3, -1, -1))
                for n, jb in enumerate(jbs):
                    a = jb - 4 * qg
                    tiles.append(dict(qg=qg, n=n, N=len(jbs), jb=jb, c0=(a * 128 if a > 0 else 0), diag=(a >= 0)))
            T = len(tiles)
            for t, tl in enumerate(tiles):
                tl['t'] = t
                tl['prev'] = tiles[t - 1] if tl['n'] > 0 else None

            def stZ(tl):
                t, qg, jb, c0 = tl['t'], tl['qg'], tl['jb'], tl['c0']
                zb = t % 4
                pbZ = banks[zb]
                P.op('pe', lambda e: e.matmul(pbZ[:, c0:512], kT[:, jb * 128:(jb + 1) * 128],
                                              qT[:, qg * 512 + c0:(qg + 1) * 512], start=True, stop=(not tl['diag'])),
                     reads=[('kT', p, jb // 4), ('qT', p, qg)], writes=[('psum', zb)])
                if tl['diag']:
                    P.op('pe', lambda e: e.matmul(pbZ[:, c0:c0 + 128], idb, NEGtri, start=False, stop=True),
                         reads=['ident_b', ('NEGm', 0)], writes=[('psum', zb)])

            def stA(tl):
                t, n, N, c0 = tl['t'], tl['n'], tl['N'], tl['c0']
                zb = t % 4
                pbZ = banks[zb]
                e_, L_ = eb[t % 3], Lb[t % 3]
                P.op('act', lambda e: e.activation(e_[:, c0:512], pbZ[:, c0:512], AF.Exp), reads=[('psum', zb)],
                     writes=[('f32', t % 3)])
                P.op('act', lambda e: e.activation(L_[:, c0:512], e_[:, c0:512], AF.Ln, bias=1.0),
                     reads=[('f32', t % 3)], writes=[('b16', t % 3)])
                if n < N - 1:
                    R_ = Rb[t % 2]
                    rk = ('b16', 5 + t % 2)
                    if n == 0:
                        P.op('dve', lambda e: e.tensor_copy(R_[:, c0:512], L_[:, c0:512]), reads=[('b16', t % 3)],
                             writes=[rk])
                    else:
                        pc0 = tl['prev']['c0']
                        Rp = Rb[(t - 1) % 2]
                        P.op('dve', lambda e: e.tensor_tensor(R_[:, pc0:512], Rp[:, pc0:512], L_[:, pc0:512],
                                                               op=ALU.add),
                             reads=[('b16', 5 + (t - 1) % 2), ('b16', t % 3)], writes=[rk])
                        if pc0 > c0:
                            P.op('dve', lambda e: e.tensor_copy(R_[:, c0:pc0], L_[:, c0:pc0]),
                                 reads=[('b16', t % 3), rk], writes=[rk])

            def stC(tl):
                t, n, c0 = tl['t'], tl['n'], tl['c0']
                zb = t % 4
                pbZ = banks[zb]
                L_ = Lb[t % 3]
                P.op('pe', lambda e: e.matmul(pbZ[:, c0:512], negTri, L_[:, c0:512], start=False, stop=(n == 0),
                                              skip_group_check=True),
                     reads=['negTri', ('b16', t % 3)], writes=[('psum', zb)])
                if n >= 1:
                    pc0 = tl['prev']['c0']
                    Rp = Rb[(t - 1) % 2]
                    P.op('pe', lambda e: e.matmul(pbZ[:, pc0:512], negOnes, Rp[:, pc0:512], start=False, stop=True,
                                                  skip_group_check=True),
                         reads=['negOnes', ('b16', 5 + (t - 1) % 2)], writes=[('psum', zb)])

            def stX(tl):
                t, n, c0 = tl['t'], tl['n'], tl['c0']
                zb = t % 4
                pbZ = banks[zb]
                w_ = wb[t % 2]
                if n == 0 and c0 > 0:
                    P.op('pool', lambda e: e.memset(w_[:, 0:c0], 0.0), reads=[('b16', 3 + t % 2)],
                         writes=[('b16', 3 + t % 2)])
                P.op('act', lambda e: e.activation(w_[:, c0:512], pbZ[:, c0:512], AF.Exp),
                     reads=[('psum', zb), ('b16', 3 + t % 2)], writes=[('b16', 3 + t % 2)])

            def stO(tl):
                t, n, N, qg, jb, c0 = tl['t'], tl['n'], tl['N'], tl['qg'], tl['jb'], tl['c0']
                ob = 4 + (qg % 2)
                pbO = banks[ob]
                oc0 = 0 if n == 0 else c0
                P.op('pe', lambda e: e.matmul(pbO[:, oc0:512], vv[:, jb, :], wb[t % 2][:, oc0:512],
                                              start=(n == 0), stop=(n == N - 1)),
                     reads=vkeys(p, jb // 4) + [('b16', 3 + t % 2)], writes=[('psum', ob)])
                if n == N - 1:
                    hk = self.hT_keys(4 * qg, 4 * qg + 4)
                    pbG = banks[6]
                    for kc in range(8):
                        P.op('pe', lambda e: e.matmul(pbG[:, :], wvg[:, kc, 128:256], hT[:, kc, qg * 512:(qg + 1) * 512],
                                                      start=(kc == 0), stop=(kc == 7)), reads=kvg + hk,
                             writes=[('psum', 6)])
                    sg, tt = self.sg[qg % 2], self.tt[qg % 2]
                    self.silu_gate(pbG[:, :], ('psum', 6), sg[:], ('sg', qg % 2))
                    P.op('dve', lambda e: e.tensor_tensor(tt[:], pbG[:, :], sg[:], op=ALU.mult),
                         reads=[('psum', 6), ('sg', qg % 2)], writes=[('tt', qg % 2)])
                    yk = [('yT', hh, i) for i in range(4 * qg, 4 * qg + 4)]
                    P.op('dve', lambda e: e.tensor_tensor(yT[:, hh, qg * 512:(qg + 1) * 512], pbO[:, :], tt[:],
                                                          op=ALU.mult),
                         reads=[('psum', ob), ('tt', qg % 2)], writes=yk)

            pieces = list(pieces)
            npc = len(pieces)
            done_pc = [0]
            stZ(tiles[0])
            for m in range(T + 2):
                if 0 <= m - 1 < T:
                    stC(tiles[m - 1])
                if m + 1 < T:
                    stZ(tiles[m + 1])
                if 0 <= m - 2 < T:
                    stO(tiles[m - 2])
                if pieces and m >= 1:
                    want = min(npc, ((m * npc) // max(1, T - 3)) + 1)
                    while pieces and done_pc[0] < want:
                        pieces.pop(0)()
                        done_pc[0] += 1
                if m < T:
                    stA(tiles[m])
                if 0 <= m - 1 < T:
                    stX(tiles[m - 1])
            while pieces:
                pieces.pop(0)()

        def wspec(hh):
            return [[(W[:, 2064 + 128 * hh:2064 + 128 * hh + 128], 128),
                     (W[:, 3088 + 128 * hh:3088 + 128 * hh + 128], 128)],
                    [(W[:, 4112 + 128 * hh:4112 + 128 * hh + 128], 128),
                     (W[:, 6160 + 128 * hh:6160 + 128 * hh + 128], 128)]]

        def load(hh):
            self.item_ctr += 1
            self.cur_issue = self.item_ctr
            ids[hh] = self.item_ctr
            return [self.load_w(parts) for parts in wspec(hh)]
        ids = {}
        ws = {0: load(0)}
        self.nrr = 8
        for piece in step1_pieces(0, ws[0], self.bank):
            piece()
        for hh in range(8):
            pieces = []
            if hh + 1 < 8:
                ws[hh + 1] = load(hh + 1)
                pieces = step1_pieces(hh + 1, ws[hh + 1], lambda: 7)
            if hh == 1 and hook is not None:
                hook()
            step2(hh, ws[hh], pieces)
            self.item_done.add(ids[hh])

    def build(self):
        d, NSEQ = self.d, self.NSEQ
        self.mark('prologue')
        self.prologue()
        self.even_consts()
        one = (self.nlayers == 1)
        for b in range(NSEQ):
            self.mark('L0 s%d frontend' % b)
            self.frontend(0, b, d["x"][b], 'xin')
            st = {}
            self.mark('L0 s%d pool' % b)
            self.even_half1(b, lambda: st.update(ow=self.load_ow(d["even_out_w"][0:1024, :])))
            if b == 0 and not one:
                self.ada_layer(1)
            self.mark('L0 s%d out1' % b)
            self.outproj(0, b, st['ow'][0], st['ow'][1], d["x"][b], 'xin', self.xs[b], 'xs', final=False)
            self.mark('L0 s%d sgu' % b)
            self.even_half2(b, lambda: st.update(ow=self.load_ow(d["even_out_w"][1024:2048, :])))
            self.mark('L0 s%d out2' % b)
            self.outproj(0, b, st['ow'][0], st['ow'][1], self.xs[b], 'xs', self.out[b] if one else self.xs[b],
                         'out' if one else 'xs', final=(one and self.final))
        if not one:
            self.P.barrier()
            self.odd_consts()
            for n in range(2):
                self.gate_bcast(self.g1row[0:NSEQ, n * 512:(n + 1) * 512], ('g1row', n), n)
            for b in range(NSEQ):
                self.mark('L1 s%d frontend' % b)
                self.frontend(1, b, self.xs[b], 'xs')
                st = {}
                self.mark('L1 s%d gla' % b)
                self.P.barrier()
                self.odd_half1(b, lambda: st.update(ow=self.load_ow(d["odd_out_w"][0:1024, :])))
                self.mark('L1 s%d out1' % b)
                self.outproj(1, b, st['ow'][0], st['ow'][1], self.xs[b], 'xs', self.xs[b], 'xs', final=False)
                self.mark('L1 s%d sb' % b)
                self.P.barrier()
                self.odd_half2(b, lambda: st.update(ow=self.load_ow(d["odd_out_w"][1024:2048, :])))
                self.mark('L1 s%d out2' % b)
                self.outproj(1, b, st['ow'][0], st['ow'][1], self.xs[b], 'xs', self.out[b], 'out', final=self.final)
        self.mark('end')
        self.P.finish()
        return self.nc


def make_in_maps(inputs, NSEQ, ncores):
    f = lambda a: np.ascontiguousarray(np.asarray(a, dtype=np.float32))
    shared = {
        "ada_w": f(inputs["ada_w"]), "ada_b": f(inputs["ada_b"]), "norm_g": f(inputs["norm_g"]),
        "even_in_w": f(inputs["even_in_w"][0]), "pool_w": f(inputs["pool_w"][0]),
        "pool_scale": f(inputs["pool_scale"][0]), "sgu_norm_g": f(inputs["sgu_norm_g"][0]),
        "sgu_wT": f(np.transpose(np.asarray(inputs["sgu_w"][0]), (0, 2, 1))),
        "sgu_b": f(np.asarray(inputs["sgu_b"][0]).reshape(-1)), "even_out_w": f(inputs["even_out_w"][0]),
        "odd_in_w": f(inputs["odd_in_w"][0]), "gla_gate_w": f(inputs["gla_gate_w"][0]),
        "gla_gate_b": f(inputs["gla_gate_b"][0]), "gla_norm_g": f(np.asarray(inputs["gla_norm_g"][0]).reshape(-1)),
        "odd_out_w": f(inputs["odd_out_w"][0]), "final_g": f(inputs["final_g"]),
    }
    x = np.asarray(inputs["x"], dtype=np.float32)
    c = np.asarray(inputs["c"], dtype=np.float32)
    maps = []
    for i in range(ncores):
        m = dict(shared)
        m["x"] = np.ascontiguousarray(x[i * NSEQ:(i + 1) * NSEQ])
        m["c"] = np.ascontiguousarray(c[i * NSEQ:(i + 1) * NSEQ])
        maps.append(m)
    return maps


def kernel(**inputs):
    ncores, NSEQ = 8, 2
    mk = MK(NSEQ=NSEQ, S=2048, nlayers=2, final=True)
    nc = mk.build()
    maps = make_in_maps(inputs, NSEQ, ncores)
    res = run_bass_kernel_spmd(nc, maps, core_ids=list(range(ncores)))
    return np.concatenate([r["out"] for r in res.results], axis=0)
```
